# Optimizing a Trainium2 kernel written in Bass

```python
import jax
import jax.numpy as jnp
from jax import lax
import numpy as np


D_MODEL = 1024
BATCH = 1
SEQ = 16384
DEPTH = 4

GRID_W = 64
CTX_LEN = 256
N_MIXERS = 3
MIX_CONV = 0
MIX_ATTN = 1
MIX_MLSTM = 2
NORM_EPS = 1e-6
FFN_HIDDEN = 2816
CONV_WIDTH = 3
ATTN_HEAD_DIM = 128
ATTN_Q_HEADS = D_MODEL // ATTN_HEAD_DIM
ATTN_KV_HEADS = 2
ATTN_GROUP = ATTN_Q_HEADS // ATTN_KV_HEADS
ATTN_Q_DIM = ATTN_Q_HEADS * ATTN_HEAD_DIM
ATTN_KV_DIM = ATTN_KV_HEADS * ATTN_HEAD_DIM
ATTN_QKV_DIM = ATTN_Q_DIM + 2 * ATTN_KV_DIM
Q_BLOCK = 128
ROPE_THETA = 10000.0
MLSTM_HEADS = 4
MLSTM_DV = D_MODEL // MLSTM_HEADS
MLSTM_DK = MLSTM_DV // 2
MLSTM_CHUNK = 64
MLSTM_KD = MLSTM_HEADS * MLSTM_DK
MLSTM_N_GATES = 4 * MLSTM_HEADS
MLSTM_KVG_DIM = MLSTM_KD + D_MODEL + MLSTM_N_GATES
MLSTM_IN_DIM = MLSTM_KVG_DIM + MLSTM_KD + D_MODEL

kernel_name = 'hybrid_conv_gqa_mlstm_dit'


def _ctx_read_at_or_after(i):
    return any((j % N_MIXERS) != MIX_CONV for j in range(i, DEPTH))


def _rms(t, g):
    t32 = t.astype(jnp.float32)
    y = t32 * lax.rsqrt(jnp.mean(t32 * t32, axis=-1, keepdims=True) + NORM_EPS)
    return y.astype(t.dtype) * g


def _modulate(t, g, shift, scale):
    return _rms(t, g) * (1 + scale) + shift


def _swiglu(h, w13, w2):
    gte, up = jnp.split(h @ w13, 2, axis=-1)
    return (jax.nn.silu(gte) * up) @ w2


def _dwconv3(t, k):
    rhs = k[:, None, :].astype(t.dtype)
    pad = CONV_WIDTH // 2
    return lax.conv_general_dilated(t, rhs, window_strides=(1,), padding=[(pad, pad)],
                                    dimension_numbers=('NWC', 'WIO', 'NWC'),
                                    feature_group_count=t.shape[-1])


def _short_conv_mixer(h, w_in, k, w_out):
    bgate, cgate, xv = jnp.split(h @ w_in, 3, axis=-1)
    return (bgate * _dwconv3(cgate * xv, k)) @ w_out


def _axial_rope_tables(n_tok, dtype):
    rows = n_tok // GRID_W
    row = jnp.repeat(jnp.arange(rows), GRID_W).astype(jnp.float32)
    col = jnp.tile(jnp.arange(GRID_W), rows).astype(jnp.float32)
    seg = ATTN_HEAD_DIM // 2
    inv = ROPE_THETA ** (-jnp.arange(seg // 2, dtype=jnp.float32) / (seg // 2))
    ang_r = row[:, None] * inv
    ang_c = col[:, None] * inv
    ang = jnp.concatenate([ang_r, ang_r, ang_c, ang_c], axis=-1)
    return jnp.cos(ang).astype(dtype), jnp.sin(ang).astype(dtype)


def _apply_rope(t, cos, sin):
    ts = t.reshape(t.shape[:-1] + (2, 2, ATTN_HEAD_DIM // 4))
    rot = jnp.stack([-ts[..., 1, :], ts[..., 0, :]], axis=-2).reshape(t.shape)
    return t * cos[:, None, :] + rot * sin[:, None, :]


def _attend(q, k, v):
    s = jnp.einsum('bqhgd,bkhd->bhgqk', q, k).astype(jnp.float32) * (ATTN_HEAD_DIM ** -0.5)
    p = jax.nn.softmax(s, axis=-1).astype(v.dtype)
    return jnp.einsum('bhgqk,bkhd->bqhgd', p, v)


def _gqa_mixer(h, hc, with_ctx_out, w_qkv, q_g, k_g, w_o, cos, sin):
    bsz, t = h.shape[:2]
    tc = hc.shape[1]
    w_q, w_kv = w_qkv[:, :ATTN_Q_DIM], w_qkv[:, ATTN_Q_DIM:]

    def proj_q(u):
        return _rms((u @ w_q).reshape(u.shape[0], u.shape[1], ATTN_Q_HEADS, ATTN_HEAD_DIM), q_g)

    def proj_kv(u):
        kv = u @ w_kv
        kk = _rms(kv[..., :ATTN_KV_DIM].reshape(u.shape[0], u.shape[1], ATTN_KV_HEADS, ATTN_HEAD_DIM), k_g)
        vv = kv[..., ATTN_KV_DIM:].reshape(u.shape[0], u.shape[1], ATTN_KV_HEADS, ATTN_HEAD_DIM)
        return kk, vv

    q = _apply_rope(proj_q(h), cos, sin)
    k, v = proj_kv(h)
    k = _apply_rope(k, cos, sin)
    kc, vc = proj_kv(hc)
    k_all = jnp.concatenate([kc, k], axis=1)
    v_all = jnp.concatenate([vc, v], axis=1)
    nblk = t // Q_BLOCK
    qb = q.reshape(bsz, nblk, Q_BLOCK, ATTN_KV_HEADS, ATTN_GROUP, ATTN_HEAD_DIM).swapaxes(0, 1)
    ob = lax.map(lambda qi: _attend(qi, k_all, v_all), qb)
    y = ob.swapaxes(0, 1).reshape(bsz, t, D_MODEL) @ w_o
    yc = None
    if with_ctx_out:
        qc = proj_q(hc).reshape(bsz, tc, ATTN_KV_HEADS, ATTN_GROUP, ATTN_HEAD_DIM)
        yc = _attend(qc, kc, vc).reshape(bsz, tc, D_MODEL) @ w_o
    return y, yc


def _mlstm_scan(q, k, v, ig, fg, state):
    with_out = q is not None
    bsz, t, nh = ig.shape
    L = MLSTM_CHUNK
    nc = t // L

    def chunks(a):
        return a.reshape((bsz, nc, L) + a.shape[2:]).swapaxes(0, 1)

    causal = jnp.tril(jnp.ones((L, L), dtype=bool))

    def step(carry, xs):
        C, n, m = carry
        if with_out:
            qc, kc, vc, ic, fc = xs
        else:
            kc, vc, ic, fc = xs
        A = jnp.cumsum(jax.nn.log_sigmoid(fc), axis=1)
        A_end = A[:, -1]
        w_end = A_end[:, None] - A + ic
        m_new = jnp.maximum(A_end + m, jnp.max(w_end, axis=1))
        e_end = jnp.exp(w_end - m_new[:, None])
        decay = jnp.exp(A_end + m - m_new)
        C_new = decay[..., None, None] * C + jnp.einsum('bsh,bshd,bshv->bhdv', e_end, kc, vc)
        n_new = decay[..., None] * n + jnp.einsum('bsh,bshd->bhd', e_end, kc)
        if not with_out:
            return (C_new, n_new, m_new), None
        Dm = A[:, :, None] - A[:, None] + ic[:, None]
        Dm = jnp.where(causal[None, :, :, None], Dm, -jnp.inf)
        inter = A + m[:, None]
        m_t = jnp.maximum(jnp.max(Dm, axis=2), inter)
        w = jnp.exp(Dm - m_t[:, :, None])
        s = jnp.einsum('bthd,bshd->btsh', qc, kc) * w
        sc = jnp.exp(inter - m_t)
        h_num = jnp.einsum('btsh,bshv->bthv', s, vc) + sc[..., None] * jnp.einsum('bthd,bhdv->bthv', qc, C)
        nq = jnp.sum(s, axis=2) + sc * jnp.einsum('bthd,bhd->bth', qc, n)
        hh = h_num / jnp.maximum(jnp.abs(nq), jnp.exp(-m_t))[..., None]
        return (C_new, n_new, m_new), hh

    xs = (chunks(k), chunks(v), chunks(ig), chunks(fg))
    if with_out:
        xs = (chunks(q),) + xs
    state, hs = lax.scan(step, state, xs)
    if with_out:
        hs = hs.swapaxes(0, 1).reshape(bsz, t, nh, MLSTM_DV)
    return state, hs


def _mlstm_split_kvg(p, b_gate):
    bsz, t = p.shape[:2]
    k = p[..., :MLSTM_KD].reshape(bsz, t, MLSTM_HEADS, MLSTM_DK).astype(jnp.float32) * (MLSTM_DK ** -0.5)
    v = p[..., MLSTM_KD:MLSTM_KD + D_MODEL].reshape(bsz, t, MLSTM_HEADS, MLSTM_DV).astype(jnp.float32)
    g = (p[..., MLSTM_KD + D_MODEL:MLSTM_KVG_DIM] + b_gate.reshape(-1)).astype(jnp.float32)
    return k, v, g.reshape(bsz, t, 4, MLSTM_HEADS)


def _mlstm_split_qo(p):
    bsz, t = p.shape[:2]
    q = p[..., MLSTM_KVG_DIM:MLSTM_KVG_DIM + MLSTM_KD].reshape(bsz, t, MLSTM_HEADS, MLSTM_DK).astype(jnp.float32)
    return q, p[..., MLSTM_KVG_DIM + MLSTM_KD:]


def _rev(a, d):
    return jnp.flip(a, axis=1) if d == 1 else a


def _mlstm_out(hh, o, norm_g, w_o):
    bsz, t = hh.shape[:2]
    hn = _rms(hh.astype(o.dtype), norm_g.reshape(MLSTM_HEADS, MLSTM_DV)).reshape(bsz, t, D_MODEL)
    return (jax.nn.sigmoid(o) * hn) @ w_o


def _mlstm_mixer(h, hc, with_ctx_out, w_in, b_gate, norm_g, w_o):
    bsz = h.shape[0]
    p = h @ w_in
    k, v, g = _mlstm_split_kvg(p, b_gate)
    q, o = _mlstm_split_qo(p)
    if with_ctx_out:
        pc = hc @ w_in
        qc, oc = _mlstm_split_qo(pc)
    else:
        pc = hc @ w_in[:, :MLSTM_KVG_DIM]
        qc, oc = None, None
    kc, vc, gc = _mlstm_split_kvg(pc, b_gate)
    zero = (jnp.zeros((bsz, MLSTM_HEADS, MLSTM_DK, MLSTM_DV), jnp.float32),
            jnp.zeros((bsz, MLSTM_HEADS, MLSTM_DK), jnp.float32),
            jnp.zeros((bsz, MLSTM_HEADS), jnp.float32))
    lat_out, ctx_out = [], []
    for d in range(2):
        ig, fg = 2 * d, 2 * d + 1
        st_ctx, y_ctx = _mlstm_scan(_rev(qc, d) if with_ctx_out else None, _rev(kc, d), _rev(vc, d),
                                    _rev(gc[:, :, ig], d), _rev(gc[:, :, fg], d), zero)
        _, y_lat = _mlstm_scan(_rev(q, d), _rev(k, d), _rev(v, d),
                               _rev(g[:, :, ig], d), _rev(g[:, :, fg], d), st_ctx)
        lat_out.append(_rev(y_lat, d))
        if with_ctx_out:
            ctx_out.append(_rev(y_ctx, d))
    y = _mlstm_out(lat_out[0] + lat_out[1], o, norm_g, w_o)
    yc = _mlstm_out(ctx_out[0] + ctx_out[1], oc, norm_g, w_o) if with_ctx_out else None
    return y, yc


def setup_inputs(seed: int = 0) -> dict:
    key = jax.random.key(seed)
    keys = list(jax.random.split(key, 32))
    f32 = jnp.float32

    def nk():
        return keys.pop()

    def w(shape, fan_in, scale=1.0):
        return jax.random.normal(nk(), shape, f32) * (scale * fan_in ** -0.5)

    n_conv = sum(1 for i in range(DEPTH) if i % N_MIXERS == MIX_CONV)
    n_attn = sum(1 for i in range(DEPTH) if i % N_MIXERS == MIX_ATTN)
    n_mlstm = sum(1 for i in range(DEPTH) if i % N_MIXERS == MIX_MLSTM)
    D = D_MODEL
    gate_bias = jnp.concatenate([
        -3.0 + 0.1 * jax.random.normal(nk(), (n_mlstm, 1, MLSTM_HEADS), f32),
        3.0 + 3.0 * jax.random.uniform(nk(), (n_mlstm, 1, MLSTM_HEADS), f32),
        -3.0 + 0.1 * jax.random.normal(nk(), (n_mlstm, 1, MLSTM_HEADS), f32),
        3.0 + 3.0 * jax.random.uniform(nk(), (n_mlstm, 1, MLSTM_HEADS), f32)], axis=1)
    return {
        'x': jax.random.normal(nk(), (BATCH, SEQ, D), f32),
        'c': jax.random.normal(nk(), (BATCH, D), f32),
        'ctx': jax.random.normal(nk(), (BATCH, CTX_LEN, D), f32),
        'c_ctx': jax.random.normal(nk(), (D,), f32),
        'mod_w': w((DEPTH, D, 9 * D), D, 0.5),
        'mod_b': 0.02 * jax.random.normal(nk(), (DEPTH, 9 * D), f32),
        'norm_g': 1.0 + 0.05 * jax.random.normal(nk(), (DEPTH, 3, D), f32),
        'ffn_w13': w((DEPTH, 2, D, 2 * FFN_HIDDEN), D),
        'ffn_w2': w((DEPTH, 2, FFN_HIDDEN, D), FFN_HIDDEN),
        'conv_w_in': w((n_conv, D, 3 * D), D),
        'conv_k': w((n_conv, CONV_WIDTH, D), CONV_WIDTH),
        'conv_w_out': w((n_conv, D, D), D),
        'attn_w_qkv': w((n_attn, D, ATTN_QKV_DIM), D),
        'attn_q_g': 1.0 + 0.05 * jax.random.normal(nk(), (n_attn, ATTN_HEAD_DIM), f32),
        'attn_k_g': 1.0 + 0.05 * jax.random.normal(nk(), (n_attn, ATTN_HEAD_DIM), f32),
        'attn_w_o': w((n_attn, D, D), D),
        'mlstm_w_in': w((n_mlstm, D, MLSTM_IN_DIM), D),
        'mlstm_b_gate': gate_bias,
        'mlstm_norm_g': 1.0 + 0.05 * jax.random.normal(nk(), (n_mlstm, D), f32),
        'mlstm_w_o': w((n_mlstm, D, D), D),
    }


def reference(x, c, ctx, c_ctx, mod_w, mod_b, norm_g, ffn_w13, ffn_w2,
              conv_w_in, conv_k, conv_w_out,
              attn_w_qkv, attn_q_g, attn_k_g, attn_w_o,
              mlstm_w_in, mlstm_b_gate, mlstm_norm_g, mlstm_w_o):
    bsz, n_tok, _ = x.shape
    cos, sin = _axial_rope_tables(n_tok, x.dtype)
    lat, cx = x, ctx
    counters = [0, 0, 0]
    for i in range(DEPTH):
        kind = i % N_MIXERS
        j = counters[kind]
        counters[kind] += 1
        ctx_out = _ctx_read_at_or_after(i + 1)
        if not _ctx_read_at_or_after(i):
            cx = None
        mod = (jax.nn.silu(c) @ mod_w[i] + mod_b[i]).reshape(bsz, 3, 3, 1, D_MODEL)
        modc = (jax.nn.silu(c_ctx) @ mod_w[i] + mod_b[i]).reshape(3, 3, D_MODEL)
        lat = lat + 0.5 * mod[:, 0, 2] * _swiglu(_modulate(lat, norm_g[i, 0], mod[:, 0, 0], mod[:, 0, 1]), ffn_w13[i, 0], ffn_w2[i, 0])
        if cx is not None:
            cx = cx + 0.5 * modc[0, 2] * _swiglu(_modulate(cx, norm_g[i, 0], modc[0, 0], modc[0, 1]), ffn_w13[i, 0], ffn_w2[i, 0])
        h = _modulate(lat, norm_g[i, 1], mod[:, 1, 0], mod[:, 1, 1])
        hc = _modulate(cx, norm_g[i, 1], modc[1, 0], modc[1, 1]) if cx is not None else None
        if kind == MIX_CONV:
            y = _short_conv_mixer(h, conv_w_in[j], conv_k[j], conv_w_out[j])
            yc = _short_conv_mixer(hc, conv_w_in[j], conv_k[j], conv_w_out[j]) if ctx_out else None
        elif kind == MIX_ATTN:
            y, yc = _gqa_mixer(h, hc, ctx_out, attn_w_qkv[j], attn_q_g[j], attn_k_g[j], attn_w_o[j], cos, sin)
        else:
            y, yc = _mlstm_mixer(h, hc, ctx_out, mlstm_w_in[j], mlstm_b_gate[j], mlstm_norm_g[j], mlstm_w_o[j])
        lat = lat + mod[:, 1, 2] * y
        if ctx_out:
            cx = cx + modc[1, 2] * yc
            cx = cx + 0.5 * modc[2, 2] * _swiglu(_modulate(cx, norm_g[i, 2], modc[2, 0], modc[2, 1]), ffn_w13[i, 1], ffn_w2[i, 1])
        else:
            cx = None
        lat = lat + 0.5 * mod[:, 2, 2] * _swiglu(_modulate(lat, norm_g[i, 2], mod[:, 2, 0], mod[:, 2, 1]), ffn_w13[i, 1], ffn_w2[i, 1])
    return lat
```

```python
import numpy as np
import ml_dtypes
from contextlib import ExitStack, contextmanager
import concourse.bass as bass
import concourse.mybir as mybir
from concourse.bass_utils import run_bass_kernel_spmd

F32 = mybir.dt.float32
BF16 = mybir.dt.bfloat16
AF = mybir.ActivationFunctionType
ALU = mybir.AluOpType
AX = mybir.AxisListType

NCORES = 8
D = 1024
SEQ = 16384
TPC = SEQ // NCORES
CTX = 256
FH = 2816
DEPTH = 4
NB = 4
BN = 512
EPS = 1e-6
GRID_W = 64
DH = 128
MH = 4
MDV = 256
MDK = 128
M_KVG = 512 + 1024 + 16
M_IN = M_KVG + 512 + 1024


class SemSlot:
    def __init__(self, name, unit=16):
        self.name = name
        self.unit = unit
        self.groups = []
        self.closed = True
        self.sem = None
        self.ends = None


class View:
    __slots__ = ("tt", "ap", "key")

    def __init__(self, tt, ap, key):
        self.tt = tt
        self.ap = ap
        self.key = key


class _Keyed:
    __slots__ = ("tt", "key")

    def __init__(self, tt, key):
        self.tt = tt
        self.key = key

    def __getitem__(self, idx):
        return View(self.tt, self.tt.h[idx], self.key)


class TT:
    def __init__(self, name, h, slot=None):
        self.name = name
        self.h = h
        self.st = {}
        self.slot = slot

    def __getitem__(self, idx):
        return View(self, self.h[idx], None)

    def K(self, key):
        return _Keyed(self, key)


class Op:
    __slots__ = ("eng", "fn", "cdeps", "ddeps", "sig", "sigval", "dma", "slot", "grp", "idx", "tt")

    def __init__(self, eng, fn):
        self.eng = eng
        self.fn = fn
        self.cdeps = {}
        self.ddeps = set()
        self.sig = False
        self.sigval = 0
        self.dma = False
        self.slot = None
        self.grp = -1
        self.idx = -1
        self.tt = None


ENGS = ("pe", "act", "dve", "pool", "sp")


class Prog:
    def __init__(self, nc):
        self.nc = nc
        self.ops = {e: [] for e in ENGS}
        self.slots = {}
        self.last_compute = {e: None for e in ENGS}
        self.bar_dma = []

    def slot(self, name, unit=16):
        if name not in self.slots:
            self.slots[name] = SemSlot(name, unit)
        return self.slots[name]

    def _dep(self, op, o, kind):
        if o is None or o is op:
            return
        if o.dma:
            if op.dma and o.slot is op.slot and o.grp == op.grp:
                return
            op.ddeps.add((o.slot, o.grp))
            if o.grp == len(o.slot.groups) - 1:
                o.slot.closed = True
            return
        if o.eng == op.eng and op.eng == "pe":
            return
        cur = op.cdeps.get(o.eng)
        if cur is None or o.idx > cur.idx:
            op.cdeps[o.eng] = o

    def _acc(self, op, v, write):
        if not isinstance(v, View):
            return
        st = v.tt.st
        key = None if v.tt.name.startswith("ps") else v.key
        if key is not None:
            keys = [key, "*"]
        else:
            keys = list(st.keys())
        for k in keys:
            ent = st.get(k)
            if ent is None:
                continue
            self._dep(op, ent[0], "W")
            if write:
                for r in ent[1]:
                    self._dep(op, r, "R")
        if write:
            if key is None:
                st.clear()
                st["*"] = [op, []]
            else:
                st[key] = [op, []]
        elif op.fn is not None:
            k = key if key is not None else "*"
            if k not in st:
                st[k] = [None, []]
            st[k][1].append(op)

    def op(self, eng, fn, reads=(), writes=(), dma_out=None):
        op = Op(eng, fn)
        lst = self.ops[eng]
        op.idx = len(lst)
        if dma_out is not None:
            op.dma = True
            slot = dma_out.tt.slot
            assert slot is not None, dma_out.tt.name
            op.slot = slot
            if slot.closed:
                slot.groups.append(0)
                slot.closed = False
                self.bar_dma.append((slot, len(slot.groups) - 1))
            op.grp = len(slot.groups) - 1
            slot.groups[op.grp] += 1
        for v in reads:
            self._acc(op, v, False)
        for v in writes:
            self._acc(op, v, True)
        if dma_out is not None:
            self._acc(op, dma_out, True)
        lst.append(op)
        if fn is not None and not op.dma:
            self.last_compute[eng] = op
        return op

    def barrier(self):
        lasts = dict(self.last_compute)
        keep = [(s, g) for (s, g) in self.bar_dma if s.name.startswith("bnc") or (s.name[:2] == "cg" and s.name[2:].isdigit())]
        dmas = [(s, g) for (s, g) in self.bar_dma if (s, g) not in keep]
        for s, g in dmas:
            if g == len(s.groups) - 1:
                s.closed = True
        self.bar_dma = [(s, g) for (s, g) in keep if g == len(s.groups) - 1 and not s.closed]
        for e in ENGS:
            op = Op(e, None)
            op.idx = len(self.ops[e])
            for e2, o in lasts.items():
                if o is not None and e2 != e:
                    op.cdeps[e2] = o
            op.ddeps = set(dmas)
            self.ops[e].append(op)

    @staticmethod
    def _a(x):
        return x.ap if isinstance(x, View) else x

    def matmul(self, out, lhsT, rhs, start=True, stop=True):
        a = self._a
        return self.op("pe", lambda e: e.matmul(a(out), a(lhsT), a(rhs), start=start, stop=stop),
                       reads=(lhsT, rhs), writes=(out,))

    def transpose(self, out, in_, ident):
        a = self._a
        return self.op("pe", lambda e: e.transpose(a(out), a(in_), a(ident)), reads=(in_, ident), writes=(out,))

    def act(self, out, in_, func, bias=None, scale=1.0, accum=None):
        a = self._a
        rd = [in_]
        kw = {}
        if bias is not None:
            kw["bias"] = a(bias)
            rd.append(bias)
        if isinstance(scale, View):
            rd.append(scale)
        kw["scale"] = a(scale)
        wr = [out]
        if accum is not None:
            kw["accum_out"] = a(accum)
            wr.append(accum)
        return self.op("act", lambda e: e.activation(out=a(out), in_=a(in_), func=func, **kw), reads=rd, writes=wr)

    def tt(self, eng, out, in0, in1, op):
        a = self._a
        return self.op(eng, lambda e: e.tensor_tensor(out=a(out), in0=a(in0), in1=a(in1), op=op),
                       reads=(in0, in1), writes=(out,))

    def ts(self, eng, out, in0, s1, s2, op0, op1=None):
        a = self._a
        rd = [in0] + [s for s in (s1, s2) if isinstance(s, View)]
        if op1 is None:
            return self.op(eng, lambda e: e.tensor_scalar(out=a(out), in0=a(in0), scalar1=a(s1), scalar2=None, op0=op0),
                           reads=rd, writes=(out,))
        return self.op(eng, lambda e: e.tensor_scalar(out=a(out), in0=a(in0), scalar1=a(s1), scalar2=a(s2),
                                                      op0=op0, op1=op1), reads=rd, writes=(out,))

    def stt(self, eng, out, in0, scalar, in1, op0, op1):
        a = self._a
        rd = [in0, in1] + ([scalar] if isinstance(scalar, View) else [])
        return self.op(eng, lambda e: e.scalar_tensor_tensor(out=a(out), in0=a(in0), scalar=a(scalar), in1=a(in1),
                                                             op0=op0, op1=op1), reads=rd, writes=(out,))

    def copy(self, eng, out, in_):
        a = self._a
        if eng == "act":
            return self.op(eng, lambda e: e.activation(out=a(out), in_=a(in_), func=AF.Identity), reads=(in_,), writes=(out,))
        return self.op(eng, lambda e: e.tensor_copy(out=a(out), in_=a(in_)), reads=(in_,), writes=(out,))

    def memset(self, eng, out, val):
        a = self._a
        return self.op(eng, lambda e: e.memset(a(out), val), writes=(out,))

    def recip(self, out, in_):
        a = self._a
        return self.op("dve", lambda e: e.reciprocal(out=a(out), in_=a(in_)), reads=(in_,), writes=(out,))

    def dma(self, eng, out, in_):
        a = self._a
        assert isinstance(out, View)
        return self.op(eng, lambda e: e.dma_start(out=a(out), in_=a(in_)), reads=(in_,), dma_out=out)

    def collective(self, out, in_):
        a = self._a
        assert out.tt.slot.unit == 1
        return self.op("pool", lambda e: e.collective_compute("AllGather", ALU.bypass,
                                                               replica_groups=[list(range(NCORES))],
                                                               ins=[a(in_).opt()], outs=[a(out).opt()]),
                       reads=(in_,), dma_out=out)

    def selfcheck(self):
        for e in ENGS:
            for op in self.ops[e]:
                for o in op.cdeps.values():
                    o.sig = True
        for e in ENGS:
            cnt = 0
            for op in self.ops[e]:
                if op.sig:
                    cnt += 1
                    op.sigval = cnt
        for s in self.slots.values():
            tot = 0
            s.ends = []
            for g in s.groups:
                tot += g
                s.ends.append(s.unit * tot)
        sem = {}
        pc = {e: 0 for e in ENGS}
        progress = True
        while progress:
            progress = False
            for e in ENGS:
                while pc[e] < len(self.ops[e]):
                    op = self.ops[e][pc[e]]
                    ok = all(sem.get(e2, 0) >= o.sigval for e2, o in op.cdeps.items()) and \
                        all(sem.get(s.name, 0) >= s.ends[g] for (s, g) in op.ddeps)
                    if not ok:
                        break
                    if op.fn is not None:
                        if op.dma:
                            sem[op.slot.name] = sem.get(op.slot.name, 0) + op.slot.unit
                        elif op.sig:
                            sem[e] = sem.get(e, 0) + 1
                    pc[e] += 1
                    progress = True
        stuck = {e: (pc[e], len(self.ops[e])) for e in ENGS if pc[e] < len(self.ops[e])}
        for e, (i, n) in stuck.items():
            op = self.ops[e][i]
            print("STUCK", e, i, n, "cdeps", {e2: (o.sigval, sem.get(e2, 0)) for e2, o in op.cdeps.items()},
                  "ddeps", [(s.name, s.ends[g], sem.get(s.name, 0)) for (s, g) in op.ddeps])
        return not stuck

    def emit(self):
        nc = self.nc
        for e in ENGS:
            for op in self.ops[e]:
                for o in op.cdeps.values():
                    o.sig = True
        for e in ENGS:
            cnt = 0
            for op in self.ops[e]:
                if op.sig:
                    cnt += 1
                    op.sigval = cnt
        with ExitStack() as es:
            sems = {e: es.enter_context(nc.semaphore("s_" + e)) for e in ENGS}
            for s in self.slots.values():
                s.sem = es.enter_context(nc.semaphore("d_" + s.name))
                tot = 0
                s.ends = []
                for g in s.groups:
                    tot += g
                    s.ends.append(s.unit * tot)
            block = es.enter_context(nc.Block())

            def mk(e):
                def body(eng):
                    waited = {}
                    for op in self.ops[e]:
                        for e2, o in op.cdeps.items():
                            v = o.sigval
                            if waited.get(e2, 0) < v:
                                eng.wait_ge(sems[e2], v)
                                waited[e2] = v
                        for (s, g) in op.ddeps:
                            v = s.ends[g]
                            if waited.get(s.name, 0) < v:
                                eng.wait_ge(s.sem, v)
                                waited[s.name] = v
                        if op.fn is not None:
                            ins = op.fn(eng)
                            if op.dma:
                                if op.slot.unit == 16:
                                    ins.then_inc(op.slot.sem, 16)
                                else:
                                    ins.then_inc(op.slot.sem)
                            elif op.sig:
                                ins.then_inc(sems[e], 1)
                return body

            block.tensor(mk("pe"))
            block.scalar(mk("act"))
            block.vector(mk("dve"))
            block.gpsimd(mk("pool"))
            block.sync(mk("sp"))


class Scope:
    def __init__(self, P):
        self.P = P
        self.es = ExitStack()

    _uid = [0]

    def sb(self, name, shape, dtype, slot=None):
        Scope._uid[0] += 1
        name = f"sb_{name}_{Scope._uid[0]}"
        h = self.es.enter_context(self.P.nc.sbuf_tensor(name, list(shape), dtype))
        return TT(name, h, self.P.slot(slot) if slot else None)

    def __enter__(self):
        return self

    def __exit__(self, *a):
        self.P.barrier()
        self.es.close()
        return False


class Builder:
    def __init__(self, stop=None):
        self.stop = stop
        self.nc = bass.Bass("TRN2", target_bir_lowering=False, num_devices=NCORES)
        self.P = Prog(self.nc)
        self.ncg = 0

    def din(self, name, shape, dt=F32):
        return self.nc.dram_tensor(name, list(shape), dt, kind="ExternalInput").ap()

    def dram(self, name, shape, dt, slot, unit=16):
        h = self.nc.dram_tensor(name, list(shape), dt).ap()
        return TT(name, h, self.P.slot(slot, unit))

    WSPEC = [
        ("ffn_w13", 8, D, 2 * FH), ("ffn_w2", 8, FH, D), ("conv_w_in", 2, D, 3 * D), ("conv_w_out", 2, D, D),
        ("attn_w_qkv", 1, D, 1536), ("attn_w_o", 1, D, D), ("mlstm_w_in", 1, D, M_IN), ("mlstm_w_o", 1, D, D),
    ]
    WORDER = [("ffn_w13", 0), ("ffn_w2", 0), ("conv_w_in", 0), ("conv_w_out", 0), ("ffn_w13", 1), ("ffn_w2", 1),
              ("ffn_w13", 2), ("ffn_w2", 2), ("attn_w_qkv", 0), ("attn_w_o", 0), ("ffn_w13", 3), ("ffn_w2", 3),
              ("ffn_w13", 4), ("ffn_w2", 4), ("mlstm_w_in", 0), ("mlstm_w_o", 0), ("ffn_w13", 5), ("ffn_w2", 5),
              ("ffn_w13", 6), ("ffn_w2", 6), ("conv_w_in", 1), ("conv_w_out", 1), ("ffn_w13", 7), ("ffn_w2", 7)]

    def gather_weights(self):
        P = self.P
        spec = {n: (nm, r, c) for n, nm, r, c in self.WSPEC}
        shards = {n: self.din(n, [nm, r // NCORES, c]) for n, nm, r, c in self.WSPEC}
        self.W = {}
        pend = []
        for i, (n, m) in enumerate(self.WORDER):
            nm, r, c = spec[n]
            b = self.dram(f"wb_{n}_{m}", [r // NCORES, c], F32, "bnc0" if i < 4 else "bnc1")
            g = self.dram(f"wg_{n}_{m}", [r, c], F32, f"cg{i}", unit=1)
            self.W[(n, m)] = g
            P.dma("sp", b[:, :], shards[n][m])
            pend.append((g, b))
        for g, b in pend:
            P.collective(g[:, :], b[:, :])

    def build(self):
        nc, P = self.nc, self.P
        I = {}
        I["xT"] = self.din("xT", [D, TPC])
        I["ctxT"] = self.din("ctxT", [D, CTX])
        I["cc_own"] = self.din("cc_own", [128, 2])
        I["mod_w"] = self.din("mod_w", [DEPTH, 128, 9 * D])
        I["modbT"] = self.din("modbT", [128, DEPTH, 72])
        I["ngT"] = self.din("ngT", [128, DEPTH, 3, 8])
        I["conv_kT"] = self.din("conv_kT", [128, 2, 3, 8])
        I["qkgT"] = self.din("qkgT", [128, 2])
        I["bgate_rep"] = self.din("bgate_rep", [128, 16])
        I["mng_rep"] = self.din("mng_rep", [128, D])
        I["cosT"] = self.din("cosT", [128, TPC])
        I["sinT"] = self.din("sinT", [128, TPC])
        I["cmat"] = self.din("cmat", [128, 6, 128])
        I["halo_sel"] = self.din("halo_sel", [128, 2, 16])
        I["scan_mask"] = self.din("scan_mask", [8, 18, 128])
        I["scan_valid"] = self.din("scan_valid", [128, 18])
        self.I = I
        self.outT = nc.dram_tensor("outT", [D, TPC], F32, kind="ExternalOutput").ap()
        self.dbgT = nc.dram_tensor("dbgT", [D, CTX], F32, kind="ExternalOutput").ap()

        with Scope(P) as G:
            self.G = G
            self.lat = [G.sb(f"lat{b}", [128, 8, BN], F32, slot=f"lat{b}") for b in range(NB)]
            self.cx = G.sb("cx", [128, 8, CTX], F32, slot="cx")
            self.cmat = G.sb("cmat", [128, 6, 128], F32, slot="cst")
            self.small = G.sb("small", [128, 512], F32, slot="cst2")
            self.onesD = G.sb("onesD", [128, 128], BF16)
            self.onesH = G.sb("onesH", [128, 128], BF16)
            self.ones1 = G.sb("ones1", [128, 128], BF16)
            self.onesF = G.sb("onesF", [128, 128], F32)
            self.identB = G.sb("identB", [128, 128], BF16)
            self.modsum = G.sb("modsum", [128, DEPTH * 144], F32)
            self.modT = G.sb("modT", [128, 72, 2], F32)
            self.dm = G.sb("dm", [128, 3, 3, 8, 2], F32)
            self.ps = []
            for i in range(6):
                h = G.es.enter_context(nc.psum_tensor(f"ps{i}", [128, 512], F32))
                self.ps.append(TT(f"ps{i}", h))
            hb = G.es.enter_context(nc.psum_tensor("psb", [128, 1024], BF16))
            self.psb = TT("psb", hb)
            hb2 = G.es.enter_context(nc.psum_tensor("psb2", [128, 1024], BF16))
            self.psb2 = TT("psb2", hb2)
            self.psi = 0

            self.phase0()
            for l in range(DEPTH):
                if self.stop in ("g", "p0"):
                    break
                if self.run_layer(l):
                    break
            self.finish()
        assert P.selfcheck(), "deadlock in recorded program"
        P.emit()
        return nc

    def psum(self):
        t = self.ps[self.psi % 6]
        self.psi += 1
        return t

    SM_NG = 0
    SM_MODB = 96
    SM_CK = 176
    SM_QKG = 224
    SM_BG = 232
    SM_HALO = 256
    SM_VALID = 288
    SM_CC = 320

    def phase0(self):
        P, I = self.P, self.I
        self.gather_weights()
        for b in range(NB):
            P.dma("sp", self.lat[b][:, :, :],
                  I["xT"].rearrange("(kc p) t -> p kc t", p=128)[:, :, b * BN:(b + 1) * BN])
        P.dma("sp", self.cx[:, :, :], I["ctxT"].rearrange("(kc p) t -> p kc t", p=128))
        P.dma("sp", self.cmat[:, :, :], I["cmat"])
        sm = self.small
        P.dma("sp", sm[:, self.SM_NG:self.SM_NG + 96], I["ngT"].rearrange("p a b c -> p (a b c)"))
        P.dma("sp", sm[:, self.SM_CK:self.SM_CK + 48], I["conv_kT"].rearrange("p a b c -> p (a b c)"))
        P.dma("sp", sm[:, self.SM_QKG:self.SM_QKG + 2], I["qkgT"])
        P.dma("sp", sm[:, self.SM_BG:self.SM_BG + 16], I["bgate_rep"])
        P.dma("sp", sm[:, self.SM_HALO:self.SM_HALO + 32], I["halo_sel"].rearrange("p a b -> p (a b)"))
        P.dma("sp", sm[:, self.SM_VALID:self.SM_VALID + 18], I["scan_valid"])
        P.dma("sp", sm[:, self.SM_CC:self.SM_CC + 2], I["cc_own"])
        P.memset("dve", self.onesD[:, :], 1.0 / 1024.0)
        P.memset("dve", self.onesH[:, :], 1.0 / 128.0)
        P.memset("dve", self.ones1[:, :], 1.0)
        P.memset("dve", self.onesF[:, :], 1.0)
        P.copy("dve", self.identB[:, :], self.cmat[:, 0, :])
        if self.stop == "g":
            return
        with Scope(P) as S:
            sc = S.sb("sc", [128, 2], F32)
            P.act(sc[:, :], sm[:, self.SM_CC:self.SM_CC + 2], AF.Silu)
            mw = [S.sb(f"mw{i}", [128, 9 * D], F32, slot=f"mw{i}") for i in range(2)]
            part = S.sb("mpart", [128, DEPTH * 144], F32)
            mg = S.sb("mg", [128, NCORES, DEPTH * 144], F32, slot="ld0")
            mb = self.dram("mod_b_", [128, DEPTH * 144], F32, "dr0")
            mgd = self.dram("mod_g_", [NCORES * 128, DEPTH * 144], F32, "cgm", unit=1)
            for l in range(DEPTH):
                w = mw[l % 2]
                P.dma("sp", w[:, :], I["mod_w"][l])
                pm = self.psum()
                for j in range(72):
                    P.matmul(pm.K(j)[:, 2 * j:2 * j + 2], w[:, j * 128:(j + 1) * 128], sc[:, :])
                P.copy("dve", part[:, l * 144:(l + 1) * 144], pm[:, 0:144])
            P.dma("sp", mb[:, :], part[:, :])
            P.collective(mgd[:, :], mb[:, :])
            P.dma("sp", mg[:, :, :], View(mgd, mgd.h.rearrange("(r p) f -> p r f", p=128), None))
            P.op("dve", lambda e: e.tensor_reduce(out=self.modsum.h[:, :], in_=mg.h[:, :, :].rearrange("p r f -> p f r"),
                                                  axis=AX.X, op=ALU.add),
                 reads=(mg[:, :, :],), writes=(self.modsum[:, :],))

    def ctx_flags(self, l):
        rd = [True, True, True, False][l]
        out = [True, True, False, False][l]
        return rd, out

    def check_stop(self, tag):
        return self.stop == tag

    def run_layer(self, l):
        ctx_rd, ctx_out = self.ctx_flags(l)
        if self.stop in ("only_attn", "only_mlstm"):
            want = 1 if self.stop == "only_attn" else 2
            if l != want:
                return False
            self.mod_phase(l)
            if want == 1:
                self.attn_mixer(l)
            else:
                self.mlstm_mixer(l)
            return True
        self.mod_phase(l)
        if self.check_stop(f"{l}m"):
            P = self.P
            P.copy("dve", self.cx[:, 0, 0:144], View(self.dm, self.dm.h[:, :, :, :, :].rearrange("p a b c d -> p (a b c d)"), None))
            return True
        if self.check_stop(f"{l}n"):
            with Scope(self.P) as S:
                hTs = self.norm_phase(S, 0, ctx_rd)
                for kc in range(8):
                    self.P.copy("dve", self.cx[:, kc, :], hTs[4][:, kc, :])
                    self.P.copy("dve", self.lat[1][:, kc, :], hTs[1][:, kc, :])
            return True
        self.ffn_sub(l, 0, ctx_rd)
        if self.check_stop(f"{l}a"):
            return True
        kind = l % 3
        if kind == 0:
            self.conv_mixer(l, l // 3, ctx_out)
        elif kind == 1:
            self.attn_mixer(l)
        else:
            self.mlstm_mixer(l)
        if self.check_stop(f"{l}b") or self.stop == "0h":
            return True
        self.ffn_sub(l, 2, ctx_out)
        if self.check_stop(f"{l}c"):
            return True
        return False

    def mod_phase(self, l):
        P, I = self.P, self.I
        sm = self.small
        P.dma("sp", sm[:, self.SM_MODB:self.SM_MODB + 72], I["modbT"][:, l, :])
        for col in range(2):
            pv = View(self.modsum, self.modsum.h[:, l * 144:(l + 1) * 144].rearrange("p (j c) -> p j c", c=2)[:, :, col], None)
            P.tt("dve", self.modT[:, :, col], pv, sm[:, self.SM_MODB:self.SM_MODB + 72], ALU.add)
        for s in range(3):
            for col in range(2):
                def mv(kindi):
                    j0 = (s * 3 + kindi) * 8
                    return self.modT[:, j0:j0 + 8, col]
                o = self.SM_NG + (l * 3 + s) * 8
                ng = sm[:, o:o + 8]
                P.stt("dve", self.dm[:, s, 0, :, col], mv(1), 1.0, ng, ALU.add, ALU.mult)
                P.copy("dve", self.dm[:, s, 1, :, col], mv(0))
                P.ts("dve", self.dm[:, s, 2, :, col], mv(2), 0.5 if s != 1 else 1.0, None, ALU.mult)

    def rsqrt_eps(self, out, in_):
        P = self.P
        P.ts("dve", out, in_, EPS, None, ALU.add)
        P.act(out, out, AF.Ln)
        P.act(out, out, AF.Exp, scale=-0.5)

    def dmv(self, s, k, fc, col):
        return self.dm[:, s, k, fc, col:col + 1]

    def blocks(self, with_ctx):
        bl = [(self.lat[b], BN, 0) for b in range(NB)]
        if with_ctx:
            bl.append((self.cx, CTX, 1))
        return bl

    def norm_phase(self, S, s, with_ctx, name="hT"):
        P = self.P
        bl = self.blocks(with_ctx)
        hTs = [S.sb(f"{name}{i}", [128, 8, n], BF16) for i, (_, n, _) in enumerate(bl)]
        with Scope(P) as S2:
            sq = [S2.sb(f"sq{i}", [128, 8, BN], BF16) for i in range(2)]
            rstd = [S2.sb(f"rstd{i}", [128, BN], F32) for i in range(2)]
            tmp = [S2.sb(f"ntmp{i}", [128, BN], F32) for i in range(3)]
            ti = 0
            for i, (src, n, col) in enumerate(bl):
                q = sq[i % 2]
                P.act(q[:, :, 0:n], src[:, :, :], AF.Square)
                pm = self.psum()
                for kc in range(8):
                    P.matmul(pm[:, 0:n], self.onesD[:, :], q[:, kc, 0:n], start=(kc == 0), stop=(kc == 7))
                r = rstd[i % 2]
                self.rsqrt_eps(r[:, 0:n], pm[:, 0:n])
                for kc in range(8):
                    t = tmp[ti % 3]
                    ti += 1
                    P.stt("dve", t[:, 0:n], src[:, kc, :], self.dmv(s, 0, kc, col), r[:, 0:n], ALU.mult, ALU.mult)
                    P.act(hTs[i].K(kc)[:, kc, :], t[:, 0:n], AF.Identity, bias=self.dmv(s, 1, kc, col))
        return hTs

    def wview(self, name, m, pat="(kc p) c -> p kc c"):
        g = self.W[(name, m)]
        return g, g.h.rearrange(pat, p=128)

    def ffn_sub(self, l, s, with_ctx):
        P = self.P
        fi = l * 2 + (0 if s == 0 else 1)
        g13, w13 = self.wview("ffn_w13", fi)
        g2, w2 = self.wview("ffn_w2", fi, "(hc p) o -> p hc o")
        bl = self.blocks(with_ctx)
        with Scope(P) as S:
            hTs = self.norm_phase(S, s, with_ctx)
            wa = [S.sb(f"wa{i}", [128, 8, 2, 512], BF16, slot=f"w{i}") for i in range(2)]
            wbb = [S.sb(f"wb{i}", [128, 4, D], BF16, slot=f"v{i}") for i in range(2)]
            gT = [S.sb(f"gT{i}", [128, 4, BN], BF16) for i in range(2)]
            sg = [S.sb(f"sg{i}", [128, BN], F32) for i in range(2)]
            slices = [(0, 4), (4, 4), (8, 4), (12, 4), (16, 4), (20, 2)]
            gi = 0
            si = 0
            for sl, (c0, nch) in enumerate(slices):
                a = wa[sl % 2]
                b2 = wbb[sl % 2]
                nco = nch * 128
                for kc in range(8):
                    P.dma("pool", a[:, kc, 0, 0:nco], View(g13, w13[:, kc, c0 * 128:c0 * 128 + nco], None))
                    P.dma("pool", a[:, kc, 1, 0:nco], View(g13, w13[:, kc, FH + c0 * 128:FH + c0 * 128 + nco], None))
                for hc in range(nch):
                    P.dma("pool", b2[:, hc, :], View(g2, w2[:, c0 + hc, :], None))
                for bi, (dst, n, col) in enumerate(bl):
                    g = gT[gi % 2]
                    gi += 1
                    for j in range(nch):
                        pg = self.psum()
                        pu = self.psum()
                        for kc in range(8):
                            P.matmul(pg[:, 0:n], a[:, kc, 0, j * 128:(j + 1) * 128], hTs[bi][:, kc, :],
                                     start=(kc == 0), stop=(kc == 7))
                        for kc in range(8):
                            P.matmul(pu[:, 0:n], a[:, kc, 1, j * 128:(j + 1) * 128], hTs[bi][:, kc, :],
                                     start=(kc == 0), stop=(kc == 7))
                        sgt = sg[si % 2]
                        si += 1
                        P.act(sgt[:, 0:n], pg[:, 0:n], AF.Silu)
                        P.tt("dve", g.K(j)[:, j, 0:n], pu[:, 0:n], sgt[:, 0:n], ALU.mult)
                    for oc in range(8):
                        po = self.psum()
                        for j in range(nch):
                            P.matmul(po[:, 0:n], b2[:, j, oc * 128:(oc + 1) * 128], g.K(j)[:, j, 0:n],
                                     start=(j == 0), stop=(j == nch - 1))
                        P.stt("dve", dst.K(oc)[:, oc, :], po[:, 0:n], self.dmv(s, 2, oc, col), dst.K(oc)[:, oc, :],
                              ALU.mult, ALU.add)

    def out_proj(self, S, wname, yTs, bl, wslot="w0"):
        P = self.P
        gw, wv = self.wview(wname, 0 if not isinstance(wname, tuple) else 0)
        wo = S.sb("wo", [128, 8, D], BF16, slot=wslot)
        P.dma("pool", wo[:, :, :], View(gw, wv, None))
        for bi, (dst, n, col) in enumerate(bl):
            for oc in range(8):
                po = self.psum()
                for kc in range(8):
                    P.matmul(po[:, 0:n], wo[:, kc, oc * 128:(oc + 1) * 128], yTs[bi][:, kc, :],
                             start=(kc == 0), stop=(kc == 7))
                P.stt("dve", dst.K(oc)[:, oc, :], po[:, 0:n], self.dmv(1, 2, oc, col), dst.K(oc)[:, oc, :],
                      ALU.mult, ALU.add)

    def conv_mixer(self, l, j, with_ctx):
        P = self.P
        sm = self.small
        gin, win = self.wview("conv_w_in", j)
        bl = self.blocks(with_ctx)

        def ck(tap, fc):
            o = self.SM_CK + (j * 3 + tap) * 8 + fc
            return sm[:, o:o + 1]
        with Scope(P) as S:
            hTs = self.norm_phase(S, 1, with_ctx)
            yTs = [S.sb(f"yT{i}", [128, 8, n], BF16) for i, (_, n, _) in enumerate(bl)]
            halo = S.sb("halo", [128, 2, 8], F32)
            with Scope(P) as S1:
                hb = S1.sb("hb", [128, 8, 2], BF16)
                P.copy("dve", hb[:, :, 0], hTs[0][:, :, 0])
                P.copy("dve", hb[:, :, 1], hTs[NB - 1][:, :, BN - 1])
                wp = [S1.sb(f"wp{i}", [128, 8, 2, 128], BF16, slot=f"w{i}") for i in range(2)]
                cxb = S1.sb("cxb", [128, 8, 2], F32)
                csb = S1.sb("csb", [128, 2], F32)
                hg = S1.sb("hg", [128, NCORES, 16], F32, slot="ld0")
                t16 = S1.sb("t16", [128, 16], F32)
                for oc in range(8):
                    w = wp[oc % 2]
                    for k in range(2):
                        c0 = (k + 1) * D + oc * 128
                        for kc in range(8):
                            P.dma("pool", w[:, kc, k, :], View(gin, win[:, kc, c0:c0 + 128], None))
                    pm = self.psum()
                    for k in range(2):
                        for kc in range(8):
                            P.matmul(pm.K(k)[:, 2 * k:2 * k + 2], w[:, kc, k, :], hb[:, kc, :], start=(kc == 0), stop=(kc == 7))
                    P.copy("act", csb[:, :], pm.K(0)[:, 0:2])
                    P.tt("dve", cxb[:, oc, :], pm.K(1)[:, 2:4], csb[:, :], ALU.mult)
                hbd = self.dram(f"halo_b{l}", [128, 16], F32, "dr0")
                hgd = self.dram(f"halo_g{l}", [NCORES * 128, 16], F32, f"cgh{l}", unit=1)
                P.dma("sp", hbd[:, :], cxb[:, :, :].ap.rearrange("p a b -> p (a b)") if False else
                      View(cxb, cxb.h[:, :, :].rearrange("p a b -> p (a b)"), None))
                P.collective(hgd[:, :], hbd[:, :])
                P.dma("sp", hg[:, :, :], View(hgd, hgd.h.rearrange("(r p) f -> p r f", p=128), None))
                for side in range(2):
                    sel = View(sm, sm.h[:, self.SM_HALO + side * 16:self.SM_HALO + side * 16 + 16]
                               .rearrange("p (r t) -> p r t", t=2), None)
                    for oc in range(8):
                        P.tt("dve", View(t16, t16.h[:, :].rearrange("p (r t) -> p r t", t=2), None),
                             hg[:, :, oc * 2:oc * 2 + 2], sel, ALU.mult)
                        P.op("dve", (lambda e, side=side, oc=oc: e.reduce_sum(out=halo.h[:, side, oc:oc + 1], in_=t16.h[:, :],
                                                                                axis=AX.X)),
                             reads=(t16[:, :],), writes=(halo.K((side, oc))[:, side, oc:oc + 1],))
            if self.stop == "0h":
                P.copy("dve", self.cx[:, 0, 0:16], View(halo, halo.h[:, :, :].rearrange("p a b -> p (a b)"), None))
                return
            with Scope(P) as S2:
                wp = [S2.sb(f"wq{i}", [128, 8, 3, 128], BF16, slot=f"w{i}") for i in range(2)]
                cxo = [S2.sb(f"cxo{i}", [128, TPC + 2], F32) for i in range(2)]
                cxc = [S2.sb(f"cxc{i}", [128, CTX + 2], F32) for i in range(2)] if with_ctx else None
                csb = [S2.sb(f"csb{i}", [128, BN], F32) for i in range(2)]
                cv = [S2.sb(f"cv{i}", [128, BN], F32) for i in range(2)]
                ci = 0
                for oc in range(8):
                    w = wp[oc % 2]
                    for k in range(3):
                        c0 = k * D + oc * 128
                        for kc in range(8):
                            P.dma("pool", w[:, kc, k, :], View(gin, win[:, kc, c0:c0 + 128], None))
                    co = cxo[oc % 2]
                    P.copy("act", co.K("hl")[:, 0:1], halo[:, 0, oc:oc + 1])
                    P.copy("act", co.K("hr")[:, TPC + 1:TPC + 2], halo[:, 1, oc:oc + 1])
                    srcs = []
                    for bi, (dst, n, col) in enumerate(bl):
                        if col == 0:
                            tgt, off = co, 1 + bi * BN
                        else:
                            tgt, off = cxc[oc % 2], 1
                            P.memset("pool", tgt.K("hl")[:, 0:1], 0.0)
                            P.memset("pool", tgt.K("hr")[:, CTX + 1:CTX + 2], 0.0)
                        pc = self.psum()
                        px = self.psum()
                        for kc in range(8):
                            P.matmul(pc[:, 0:n], w[:, kc, 1, :], hTs[bi][:, kc, :], start=(kc == 0), stop=(kc == 7))
                        for kc in range(8):
                            P.matmul(px[:, 0:n], w[:, kc, 2, :], hTs[bi][:, kc, :], start=(kc == 0), stop=(kc == 7))
                        cs = csb[ci % 2]
                        ci += 1
                        P.copy("act", cs[:, 0:n], pc[:, 0:n])
                        P.tt("dve", tgt.K(bi)[:, off:off + n], px[:, 0:n], cs[:, 0:n], ALU.mult)
                        srcs.append((tgt, off))
                    for bi, (dst, n, col) in enumerate(bl):
                        tgt, off = srcs[bi]
                        c = cv[ci % 2]
                        ci += 1
                        P.ts("pool", c[:, 0:n], tgt[:, off - 1:off - 1 + n], ck(0, oc), None, ALU.mult)
                        P.stt("dve", c[:, 0:n], tgt[:, off:off + n], ck(1, oc), c[:, 0:n], ALU.mult, ALU.add)
                        P.stt("dve", c[:, 0:n], tgt[:, off + 1:off + 1 + n], ck(2, oc), c[:, 0:n], ALU.mult, ALU.add)
                        pb = self.psum()
                        for kc in range(8):
                            P.matmul(pb[:, 0:n], w[:, kc, 0, :], hTs[bi][:, kc, :], start=(kc == 0), stop=(kc == 7))
                        P.tt("dve", yTs[bi].K(oc)[:, oc, :], pb[:, 0:n], c[:, 0:n], ALU.mult)
            with Scope(P) as S3:
                wo = self.load_wo(S3, "conv_w_out", j)
                self.proj_out(wo, yTs, bl)

    def proj_out(self, wo, yTs, bl, rhsf=None):
        P = self.P
        for bi, (dst, n, col) in enumerate(bl):
            for oc in range(8):
                po = self.psum()
                for kc in range(8):
                    rhs = yTs[bi][:, kc, :] if rhsf is None else rhsf(bi, kc)
                    P.matmul(po[:, 0:n], wo[:, kc, oc * 128:(oc + 1) * 128], rhs,
                             start=(kc == 0), stop=(kc == 7))
                P.stt("dve", dst.K(oc)[:, oc, :], po[:, 0:n], self.dmv(1, 2, oc, col), dst.K(oc)[:, oc, :],
                      ALU.mult, ALU.add)

    def load_wo(self, S, name, m=0):
        gw, wv = self.wview(name, m)
        wo = S.sb("wo", [128, 8, D], BF16, slot="w0")
        for kc in range(8):
            self.P.dma("pool", wo[:, kc, :], View(gw, wv[:, kc, :], None))
        return wo

    def attn_mixer(self, l):
        P, I = self.P, self.I
        sm = self.small
        gq, wqkv = self.wview("attn_w_qkv", 0)
        bl = self.blocks(True)
        SCALE = DH ** -0.5
        with Scope(P) as S:
            qT = S.sb("qT", [128, 8, TPC], BF16)
            qcT = S.sb("qcT", [128, 8, CTX], BF16)
            kcT = S.sb("kcT", [128, 2, CTX], BF16)
            vc = S.sb("vc", [128, 2, 256], BF16)
            kb = self.dram("att_kb", [2 * 128, TPC // 2], F32, "dr0")
            vb = self.dram("att_vb", [TPC, 128], F32, "dr1")
            kg = self.dram("att_kg", [NCORES * 256, TPC // 2], F32, "cgk", unit=1)
            vg = self.dram("att_vg", [NCORES * TPC, 128], F32, "cgv", unit=1)
            kbf, vbf, kgf, vgf = kb.h, vb.h, kg.h, vg.h
            for t_ in (kb, vb, kg, vg):
                t_.h = t_.h.bitcast(BF16)
            with Scope(P) as S1:
                hTs = self.norm_phase(S1, 1, True)
                wq = [S1.sb(f"wq{i}", [128, 8, 512], BF16, slot=f"w{i}") for i in range(2)]
                cs = [S1.sb(f"cs{i}", [128, 2, BN], F32, slot=f"ld{i}") for i in range(2)]
                sq = [S1.sb(f"sq{i}", [128, BN], BF16) for i in range(2)]
                rs = [S1.sb(f"rs{i}", [128, BN], F32) for i in range(2)]
                qn = [S1.sb(f"qn{i}", [128, BN], F32) for i in range(2)]
                t2 = [S1.sb(f"t2{i}", [128, BN], F32) for i in range(2)]
                kst = [S1.sb(f"kst{i}", [128, BN], BF16) for i in range(2)]
                vst = [S1.sb(f"vst{i}", [128, 256], BF16) for i in range(2)]
                ui = 0
                ci = 0
                vi = 0
                for pc in range(3):
                    w = wq[pc % 2]
                    for kc in range(8):
                        P.dma("pool", w[:, kc, :], View(gq, wqkv[:, kc, pc * 512:(pc + 1) * 512], None))
                    for bi, (src, n, col) in enumerate(bl):
                        c_ = None
                        if col == 0:
                            c_ = cs[ci % 2]
                            ci += 1
                            P.dma("sp", c_[:, 0, :], I["cosT"][:, bi * BN:(bi + 1) * BN])
                            P.dma("sp", c_[:, 1, :], I["sinT"][:, bi * BN:(bi + 1) * BN])
                        nheads = 4 if pc < 2 else 2
                        for hh in range(nheads):
                            u = ui % 2
                            ui += 1
                            pq = self.psum()
                            for kc in range(8):
                                P.matmul(pq[:, 0:n], w[:, kc, hh * 128:(hh + 1) * 128], hTs[bi][:, kc, :],
                                         start=(kc == 0), stop=(kc == 7))
                            P.act(sq[u][:, 0:n], pq[:, 0:n], AF.Square)
                            pm = self.psum()
                            P.matmul(pm[:, 0:n], self.onesH[:, :], sq[u][:, 0:n])
                            self.rsqrt_eps(rs[u][:, 0:n], pm[:, 0:n])
                            go = self.SM_QKG + (0 if pc < 2 else 1)
                            P.stt("dve", qn[u][:, 0:n], pq[:, 0:n], sm[:, go:go + 1], rs[u][:, 0:n], ALU.mult, ALU.mult)
                            if col == 0:
                                pr = self.psum()
                                P.matmul(pr[:, 0:n], self.cmat[:, 1, :], qn[u][:, 0:n])
                                P.tt("dve", t2[u][:, 0:n], pr[:, 0:n], c_[:, 1, :], ALU.mult)
                                P.tt("pool", qn[u][:, 0:n], qn[u][:, 0:n], c_[:, 0, :], ALU.mult)
                                if pc < 2:
                                    h8 = pc * 4 + hh
                                    P.tt("pool", qT.K((h8, bi))[:, h8, bi * BN:(bi + 1) * BN], qn[u][:, 0:n], t2[u][:, 0:n], ALU.add)
                                else:
                                    P.tt("pool", kst[u][:, 0:n], qn[u][:, 0:n], t2[u][:, 0:n], ALU.add)
                                    P.dma("sp", kb.K((hh, bi))[hh * 128:(hh + 1) * 128, bi * BN:(bi + 1) * BN], kst[u][:, 0:n])
                            else:
                                if pc < 2:
                                    P.copy("pool", qcT.K(pc * 4 + hh)[:, pc * 4 + hh, :], qn[u][:, 0:n])
                                else:
                                    P.copy("pool", kcT.K(hh)[:, hh, :], qn[u][:, 0:n])
                        if pc == 2:
                            for st in range(n // 128):
                                pv = self.psum()
                                for kc in range(8):
                                    P.matmul(pv[:, 0:256], hTs[bi][:, kc, st * 128:(st + 1) * 128], w[:, kc, 256:512],
                                             start=(kc == 0), stop=(kc == 7))
                                if col == 0:
                                    vs = vst[vi % 2]
                                    vi += 1
                                    P.copy("act", vs[:, :], pv[:, 0:256])
                                    r0 = (bi * 4 + st) * 128
                                    P.dma("sp", vb.K(r0)[r0:r0 + 128, :], vs[:, :])
                                else:
                                    P.copy("act", vc.K(st)[:, st, :], pv[:, 0:256])
            P.collective(View(kg, kgf, None), View(kb, kbf, None))
            P.collective(View(vg, vgf, None), View(vb, vbf, None))
            with Scope(P) as S2:
                Kall = S2.sb("Kall", [128, CTX + SEQ], BF16, slot="ld0")
                Vall = S2.sb("Vall", [128, 130, 128], BF16, slot="ld1")
                pT = [S2.sb(f"pT{i}", [128, 512], BF16) for i in range(4)]
                rd = [S2.sb(f"rd{i}", [128, 512], F32) for i in range(2)]
                vgv = vg.h.rearrange("(c p) f -> p c f", p=128)
                for kvh in range(2):
                    P.copy("pool", Kall.K("c")[:, 0:CTX], kcT[:, kvh, :])
                    for r in range(NCORES):
                        P.dma("sp", Kall.K(r)[:, CTX + r * TPC:CTX + (r + 1) * TPC],
                              View(kg, kg.h[r * 256 + kvh * 128:r * 256 + (kvh + 1) * 128, :], None))
                    P.copy("pool", Vall.K("c")[:, 0:2, :], vc[:, :, kvh * 128:(kvh + 1) * 128])
                    for c0 in range(0, 128, 8):
                        P.dma("sp", Vall.K(c0)[:, 2 + c0:2 + c0 + 8, :],
                              View(vg, vgv[:, c0:c0 + 8, kvh * 128:(kvh + 1) * 128], None))
                        P.op("sp", None, reads=(Vall.K(c0)[:, 2 + c0:2 + c0 + 8, :],))
                    for qb in range(16 + 2):
                        if qb < 16:
                            qv = qT.K(("o", kvh, qb))[:, kvh * 4:(kvh + 1) * 4, qb * 128:(qb + 1) * 128]
                            nkc = 130
                        else:
                            c = qb - 16
                            qv = qcT.K(("o", kvh, c))[:, kvh * 4:(kvh + 1) * 4, c * 128:(c + 1) * 128]
                            nkc = 2
                        qr = View(qv.tt, qv.ap, None)
                        po = self.ps[qb % 2]
                        pd = self.ps[2 + qb % 2]
                        for kc in range(nkc):
                            ps_ = self.ps[4 + kc % 2]
                            P.matmul(View(ps_, ps_.h[:, :].rearrange("p (a b) -> p a b", a=4), None),
                                     Kall[:, kc * 128:(kc + 1) * 128], qr)
                            p = pT[kc % 4]
                            P.act(p[:, :], ps_[:, :], AF.Exp, scale=SCALE)
                            P.matmul(po[:, :], Vall[:, kc, :], p[:, :], start=(kc == 0), stop=(kc == nkc - 1))
                            P.matmul(pd[:, :], self.ones1[:, :], p[:, :], start=(kc == 0), stop=(kc == nkc - 1))
                        r_ = rd[qb % 2]
                        P.copy("act", r_[:, :], pd[:, :])
                        P.recip(r_[:, :], r_[:, :])
                        P.tt("dve", qv, View(po, po.h[:, :].rearrange("p (a b) -> p a b", a=4), None),
                             View(r_, r_.h[:, :].rearrange("p (a b) -> p a b", a=4), None), ALU.mult)
            with Scope(P) as S3:
                wo = self.load_wo(S3, "attn_w_o")

                def rhsf(bi, kc):
                    if bi < NB:
                        return qT[:, kc, bi * BN:(bi + 1) * BN]
                    return qcT[:, kc, :]
                self.proj_out(wo, None, bl, rhsf)

    def mlstm_mixer(self, l):
        P, I = self.P, self.I
        sm = self.small
        gin, win = self.wview("mlstm_w_in", 0)
        NCH = 18
        KS = MDK ** -0.5
        bl = self.blocks(True)

        def chunk_src(c):
            if c < 2:
                return NB, c * 128
            return (c - 2) // 4, ((c - 2) % 4) * 128
        with Scope(P) as S:
            kT = S.sb("kT", [128, 4, NCH * 128], BF16)
            qT = S.sb("qT", [128, 4, TPC], BF16)
            Gt = S.sb("Gt", [128, NCH, 16], F32)
            vd = self.dram("ml_v", [NCH * 128, 4 * 257], BF16, "dr0")
            sod = self.dram("ml_so", [TPC, D], BF16, "dr1")
            hfd = self.dram("ml_hf", [TPC, D], F32, "dr2")
            with Scope(P) as S1:
                hTs = self.norm_phase(S1, 1, True)
                wp = [S1.sb(f"wp{i}", [128, 8, 512], BF16, slot=f"w{i}") for i in range(2)]
                wg = S1.sb("wg", [128, 8, 16], BF16, slot="v0")
                vt = [S1.sb(f"vt{i}", [128, 2, 257], BF16) for i in range(2)]
                sot = [S1.sb(f"sot{i}", [128, 512], BF16) for i in range(2)]
                for t in vt:
                    P.memset("pool", t[:, :, 256:257], 1.0)
                for kc in range(8):
                    P.dma("pool", wg[:, kc, :], View(gin, win[:, kc, 1536:1552], None))
                pieces = [("k", 0), ("q", 1552), ("v0", 512), ("v1", 1024), ("o0", 2064), ("o1", 2576)]
                ti = 0
                for pi, (kind, c0) in enumerate(pieces):
                    w = wp[pi % 2]
                    for kc in range(8):
                        P.dma("pool", w[:, kc, :], View(gin, win[:, kc, c0:c0 + 512], None))
                    if kind in ("k", "q"):
                        for bi, (src, n, col) in enumerate(bl):
                            if kind == "q" and col == 1:
                                continue
                            for hh in range(4):
                                pk = self.psum()
                                for kc in range(8):
                                    P.matmul(pk[:, 0:n], w[:, kc, hh * 128:(hh + 1) * 128], hTs[bi][:, kc, :],
                                             start=(kc == 0), stop=(kc == 7))
                                if kind == "k":
                                    off = 0 if col == 1 else 256 + bi * BN
                                    P.act(kT.K((hh, bi))[:, hh, off:off + n], pk[:, 0:n], AF.Identity, scale=KS)
                                else:
                                    P.copy("act", qT.K((hh, bi))[:, hh, bi * BN:(bi + 1) * BN], pk[:, 0:n])
                    else:
                        for c in range(NCH):
                            if kind[0] == "o" and c < 2:
                                continue
                            bi, off = chunk_src(c)
                            pv = self.psum()
                            for kc in range(8):
                                P.matmul(pv[:, :], hTs[bi][:, kc, off:off + 128], w[:, kc, :], start=(kc == 0), stop=(kc == 7))
                            half = int(kind[1])
                            if kind[0] == "v":
                                t = vt[ti % 2]
                                ti += 1
                                P.copy("act", t[:, :, 0:256], View(pv, pv.h[:, :].rearrange("p (a b) -> p a b", a=2), None))
                                P.dma("sp", vd.K((c, half))[c * 128:(c + 1) * 128, half * 514:(half + 1) * 514],
                                      View(t, t.h[:, :, :].rearrange("p a b -> p (a b)"), None))
                                if half == 0:
                                    pg = self.psum()
                                    for kc in range(8):
                                        P.matmul(pg[:, 0:16], hTs[bi][:, kc, off:off + 128], wg[:, kc, :],
                                                 start=(kc == 0), stop=(kc == 7))
                                    P.tt("dve", Gt.K(c)[:, c, :], pg[:, 0:16], sm[:, self.SM_BG:self.SM_BG + 16], ALU.add)
                            else:
                                t = sot[ti % 2]
                                ti += 1
                                P.act(t[:, :], pv[:, :], AF.Sigmoid)
                                r0 = (c - 2) * 128
                                P.dma("sp", sod.K((c, half))[r0:r0 + 128, half * 512:(half + 1) * 512], t[:, :])
            with Scope(P) as S2:
                wo = self.load_wo(S2, "mlstm_w_o")
                mng = S2.sb("mng", [128, D], F32, slot="ld2")
                P.dma("sp", mng[:, :], I["mng_rep"])
                mk = S2.sb("mk", [8, 18, 128], F32, slot="ld3")
                P.dma("sp", mk[:, :, :], I["scan_mask"])
                a8 = S2.sb("a8", [128, NCH, 8], F32)
                AA = S2.sb("AA", [128, NCH, 16], F32)
                bias8 = S2.sb("bias8", [128, NCH, 8], F32)
                e8 = S2.sb("e8", [128, NCH, 8], F32)
                xA8 = S2.sb("xA8", [128, NCH, 8], F32)
                dec8 = S2.sb("dec8", [128, NCH, 8], F32)
                Atot = S2.sb("Atot", [128, 8], F32)
                Cst = S2.sb("Cst", [128, 8, 257], F32)
                Cbf = S2.sb("Cbf", [128, 8, 257], BF16)
                Cctx = S2.sb("Cctx", [128, 8, 257], F32)
                Cg = S2.sb("Cg", [128, 8, 257], F32, slot="ld0")
                At = S2.sb("At", [8, 8], F32, slot="ld1")
                Wt = S2.sb("Wt", [128, 18, 4], F32)
                abc = [S2.sb(f"abc{i}", [128, 128], F32) for i in range(2)]
                Dt = [S2.sb(f"Dt{i}", [128, 128], F32) for i in range(2)]
                Ws = [S2.sb(f"Ws{i}", [128, 128], BF16) for i in range(2)]
                ke = [S2.sb(f"ke{i}", [128, 128], BF16) for i in range(2)]
                Hs = [S2.sb(f"Hs{i}", [128, 257], F32) for i in range(2)]
                dn = [S2.sb(f"dn{i}", [128, 2], F32) for i in range(2)]
                hfw = [S2.sb(f"hfw{i}", [128, 256], F32) for i in range(2)]
                vx = [S2.sb(f"vx{i}", [128, 4, 257], BF16, slot=f"lv{i}") for i in range(3)]
                htot = S2.sb("htot", [128, D], F32)
                hf = S2.sb("hf", [128, D], F32, slot="ld4")
                so = S2.sb("so", [128, D], BF16, slot="ld5")
                yk = S2.sb("yk", [128, D], BF16)
                yTc = S2.sb("yTc", [128, 8, 128], BF16)
                junk = S2.sb("junk", [128, 256], F32)
                ssq = S2.sb("ssq", [128, 4], F32)
                csb = self.dram("ml_csb", [8 * 128, 257], F32, "dr3")
                csg = self.dram("ml_csg", [NCORES * 8 * 128, 257], F32, "cgc", unit=1)
                atb = self.dram("ml_atb", [1, 8], F32, "dr4")
                atg = self.dram("ml_atg", [NCORES, 8], F32, "cga", unit=1)
                P.act(a8[:, :, 0:4], Gt[:, :, 4:8], AF.Exp, scale=-1.0)
                P.act(a8[:, :, 4:8], Gt[:, :, 12:16], AF.Exp, scale=-1.0)
                P.ts("dve", a8[:, :, :], a8[:, :, :], 1.0, None, ALU.add)
                P.act(a8[:, :, :], a8[:, :, :], AF.Ln)
                P.ts("dve", a8[:, :, :], a8[:, :, :], -1.0, None, ALU.mult)
                pa = self.ps[1]
                for c in range(NCH):
                    P.matmul(pa.K((c, 0))[:, c * 16:c * 16 + 4], self.cmat[:, 2, :], a8[:, c, 0:4])
                    P.matmul(pa.K((c, 1))[:, c * 16 + 4:c * 16 + 8], self.cmat[:, 3, :], a8[:, c, 4:8])
                    P.matmul(pa.K((c, 2))[:, c * 16 + 8:c * 16 + 16], self.onesF[:, :], a8[:, c, :])
                P.copy("dve", View(AA, AA.h[:, :, :].rearrange("p c k -> p (c k)"), None), pa[:, 0:NCH * 16])
                P.tt("dve", bias8[:, :, 0:4], Gt[:, :, 0:4], AA[:, :, 0:4], ALU.subtract)
                P.tt("dve", bias8[:, :, 4:8], Gt[:, :, 8:12], AA[:, :, 4:8], ALU.subtract)
                P.tt("dve", e8[:, :, :], AA[:, :, 8:16], bias8[:, :, :], ALU.add)
                P.act(e8[:, :, :], e8[:, :, :], AF.Exp)
                P.act(xA8[:, :, :], AA[:, :, 0:8], AF.Exp)
                P.act(dec8[:, :, :], AA[:, :, 8:16], AF.Exp)
                P.op("dve", lambda e: e.tensor_reduce(out=Atot.h[:, :], in_=AA.h[:, 2:NCH, 8:16].rearrange("p c k -> p k c"),
                                                      axis=AX.X, op=ALU.add),
                     reads=(AA[:, :, :],), writes=(Atot[:, :],))
                P.memset("dve", Cst[:, :, :], 0.0)
                P.memset("pool", Cbf[:, :, :], 0.0)
                cnt = [0]
                vxi = [0]

                def load_vx(c):
                    t = vx[vxi[0] % 3]
                    vxi[0] += 1
                    P.dma("sp", View(t, t.h[:, :, :].rearrange("p a b -> p (a b)"), None), vd[c * 128:(c + 1) * 128, :])
                    return t

                def step(c, d, hh, vxt, outputs, inter=True):
                    col = d * 4 + hh
                    u = cnt[0] % 2
                    cnt[0] += 1
                    kch = kT[:, hh, c * 128:(c + 1) * 128]
                    if outputs:
                        qch = qT[:, hh, (c - 2) * 128:(c - 1) * 128]
                        P.ts("dve", abc[u][:, :], self.onesF[:, :], a8[:, c, col:col + 1], None, ALU.mult)
                        pA = self.ps[1]
                        P.matmul(pA.K("A")[:, 0:128], abc[u][:, :], self.cmat[:, 2 + d, :], start=True, stop=False)
                        P.matmul(pA.K("A")[:, 0:128], self.cmat[:, 0, :], self.cmat[:, 4 + d, :], start=False, stop=True)
                        P.act(Dt[u][:, :], pA.K("A")[:, 0:128], AF.Exp, bias=bias8[:, c, col:col + 1])
                        pS = self.ps[5]
                        P.matmul(pS[:, 0:128], kch, qch)
                        P.tt("dve", Ws[u][:, :], pS[:, 0:128], Dt[u][:, :], ALU.mult)
                        pH = self.ps[2]
                        P.matmul(pH[:, 0:257], Ws[u][:, :], vxt[:, hh, :])
                        P.copy("act", Hs[u][:, :], pH[:, 0:257])
                        if inter:
                            pI = self.ps[3]
                            P.matmul(pI[:, 0:257], qch, Cbf[:, col, :])
                            P.stt("dve", Hs[u][:, :], pI[:, 0:257], xA8[:, c, col:col + 1], Hs[u][:, :], ALU.mult, ALU.add)
                        P.act(dn[u][:, 0:1], Hs[u][:, 256:257], AF.Abs)
                        P.ts("dve", dn[u][:, 0:1], dn[u][:, 0:1], 1.0, None, ALU.max)
                        P.recip(dn[u][:, 1:2], dn[u][:, 0:1])
                        if d == 0:
                            P.ts("dve", hfw[u][:, :], Hs[u][:, 0:256], dn[u][:, 1:2], None, ALU.mult)
                            r0 = (c - 2) * 128
                            P.dma("sp", hfd.K((c, hh))[r0:r0 + 128, hh * 256:(hh + 1) * 256], hfw[u][:, :])
                        else:
                            P.ts("dve", htot.K(hh)[:, hh * 256:(hh + 1) * 256], Hs[u][:, 0:256], dn[u][:, 1:2], None, ALU.mult)
                    pk = self.psb2
                    P.transpose(pk.K(hh)[:, hh * 128:(hh + 1) * 128], kch, self.identB[:, :])
                    P.ts("dve", ke[u][:, :], pk.K(hh)[:, hh * 128:(hh + 1) * 128], e8[:, c, col:col + 1], None, ALU.mult)
                    pC = self.ps[0]
                    P.matmul(pC[:, 0:257], ke[u][:, :], vxt[:, hh, :])
                    P.stt("dve", Cst.K(col)[:, col, :], Cst.K(col)[:, col, :], dec8[:, c, col:col + 1], pC[:, 0:257],
                          ALU.mult, ALU.add)
                    P.copy("act", Cbf.K(col)[:, col, :], Cst.K(col)[:, col, :])

                orders = {0: list(range(NCH)), 1: [1, 0] + list(range(NCH - 1, 1, -1))}
                for i in range(NCH):
                    for d in range(2):
                        c = orders[d][i]
                        vxt = load_vx(c)
                        for hh in range(4):
                            step(c, d, hh, vxt, False)
                    if i == 1:
                        P.copy("dve", Cctx[:, :, :], Cst[:, :, :])
                        P.memset("dve", Cst[:, :, :], 0.0)
                P.dma("sp", View(csb, csb.h.rearrange("(c p) f -> p c f", p=128), None), Cst[:, :, :])
                P.dma("sp", atb[:, :], Atot[0:1, :])
                P.collective(csg[:, :], csb[:, :])
                P.collective(atg[:, :], atb[:, :])
                P.dma("sp", At[:, :], atg[:, :])
                pw = self.ps[4]
                for d in range(2):
                    for x in range(9):
                        k = d * 9 + x
                        P.matmul(pw.K(k)[:, k * 4:k * 4 + 4], mk[:, k, :], At[:, d * 4:(d + 1) * 4])
                P.act(View(Wt, Wt.h[:, :, :].rearrange("p a b -> p (a b)"), None), pw[:, 0:72], AF.Exp)
                for hh in range(4):
                    P.tt("dve", Wt[:, :, hh], Wt[:, :, hh], sm[:, self.SM_VALID:self.SM_VALID + 18], ALU.mult)
                csv = csg.h.rearrange("(r c p) f -> p r c f", c=8, p=128)
                for col in range(8):
                    d, hh = divmod(col, 4)
                    P.dma("sp", Cg[:, :, :], View(csg, csv[:, :, col, :], None))
                    P.ts("dve", Cst.K(col)[:, col, :], Cctx[:, col, :], Wt[:, d * 9 + 8, hh:hh + 1], None, ALU.mult)
                    for rp in range(NCORES):
                        P.stt("dve", Cst.K(col)[:, col, :], Cg[:, rp, :], Wt[:, d * 9 + rp, hh:hh + 1], Cst.K(col)[:, col, :],
                              ALU.mult, ALU.add)
                    P.copy("act", Cbf.K(col)[:, col, :], Cst.K(col)[:, col, :])
                for c in range(2, NCH):
                    vxt = load_vx(c)
                    for hh in range(4):
                        step(c, 0, hh, vxt, True)
                for c in range(NCH - 1, 1, -1):
                    vxt = load_vx(c)
                    for hh in range(4):
                        step(c, 1, hh, vxt, True)
                    r0 = (c - 2) * 128
                    P.dma("sp", hf[:, :], hfd[r0:r0 + 128, :])
                    P.dma("sp", so[:, :], sod[r0:r0 + 128, :])
                    P.tt("pool", htot[:, :], htot[:, :], hf[:, :], ALU.add)
                    P.memset("pool", ssq[:, :], 0.0)
                    for hh in range(4):
                        P.act(junk[:, :], htot[:, hh * 256:(hh + 1) * 256], AF.Square, accum=ssq.K(hh)[:, hh:hh + 1])
                    P.ts("dve", ssq[:, :], ssq[:, :], 1.0 / 256.0, EPS, ALU.mult, ALU.add)
                    P.act(ssq[:, :], ssq[:, :], AF.Ln)
                    P.act(ssq[:, :], ssq[:, :], AF.Exp, scale=-0.5)
                    for hh in range(4):
                        P.ts("dve", htot.K(hh)[:, hh * 256:(hh + 1) * 256], htot[:, hh * 256:(hh + 1) * 256], ssq[:, hh:hh + 1],
                             None, ALU.mult)
                    P.tt("pool", htot[:, :], htot[:, :], mng[:, :], ALU.mult)
                    P.tt("dve", yk[:, :], htot[:, :], so[:, :], ALU.mult)
                    pt = self.psb
                    for fc in range(8):
                        P.transpose(pt.K(fc)[:, fc * 128:(fc + 1) * 128], yk[:, fc * 128:(fc + 1) * 128], self.identB[:, :])
                    P.copy("act", View(yTc, yTc.h[:, :, :].rearrange("p a b -> p (a b)"), None), pt[:, :])
                    bi, off = chunk_src(c)
                    dst = self.lat[bi]
                    for oc in range(8):
                        po = self.ps[3 + oc % 2]
                        for fc in range(8):
                            P.matmul(po[:, 0:128], wo[:, fc, oc * 128:(oc + 1) * 128], yTc[:, fc, :],
                                     start=(fc == 0), stop=(fc == 7))
                        P.stt("dve", dst.K(oc)[:, oc, off:off + 128], po[:, 0:128], self.dmv(1, 2, oc, 0),
                              dst.K(oc)[:, oc, off:off + 128], ALU.mult, ALU.add)


    def finish(self):
        P = self.P
        out_tt = TT("outT", self.outT, P.slot("out"))
        dbg_tt = TT("dbgT", self.dbgT, P.slot("dbg"))
        ov = self.outT.rearrange("(kc p) t -> p kc t", p=128)
        for b in range(NB):
            P.dma("sp", View(out_tt, ov[:, :, b * BN:(b + 1) * BN], b), self.lat[b][:, :, :])
        P.dma("sp", View(dbg_tt, self.dbgT.rearrange("(kc p) t -> p kc t", p=128), None), self.cx[:, :, :])
        P.op("sp", None, reads=(View(out_tt, None, None), View(dbg_tt, None, None)))


def _consts():
    ident = np.eye(128, dtype=np.float32)
    rot = np.zeros((128, 128), np.float32)
    for d in range(128):
        a, r = divmod(d, 64)
        b, c = divmod(r, 32)
        if b == 0:
            rot[a * 64 + 32 + c, d] = -1.0
        else:
            rot[a * 64 + c, d] = 1.0
    s = np.arange(128)[:, None]
    t = np.arange(128)[None, :]
    triU = (s <= t).astype(np.float32)
    triL = (s >= t).astype(np.float32)
    negU = np.where(s <= t, 0.0, -30000.0).astype(np.float32)
    negL = np.where(s >= t, 0.0, -30000.0).astype(np.float32)
    return np.ascontiguousarray(np.stack([ident, rot, triU, triL, negU, negL], axis=1).astype(np.float32))


def _rope_tables():
    rows = SEQ // GRID_W
    row = np.repeat(np.arange(rows), GRID_W).astype(np.float32)
    col = np.tile(np.arange(GRID_W), rows).astype(np.float32)
    seg = DH // 2
    inv = (np.float32(10000.0) ** (-np.arange(seg // 2, dtype=np.float32) / np.float32(seg // 2))).astype(np.float32)
    ang_r = row[:, None] * inv
    ang_c = col[:, None] * inv
    ang = np.concatenate([ang_r, ang_r, ang_c, ang_c], axis=-1).astype(np.float32)
    return np.cos(ang).astype(np.float32), np.sin(ang).astype(np.float32)


def _fm(v):
    v = np.asarray(v, np.float32)
    lead = v.shape[:-1]
    r = v.reshape(lead + (8, 128))
    r = np.moveaxis(r, -1, 0)
    return np.ascontiguousarray(r)


_CACHE = {}


def make_in_maps(inp):
    f = lambda k: np.ascontiguousarray(np.asarray(inp[k], np.float32))
    x = f("x")[0]
    ctx = f("ctx")[0]
    cc = np.stack([f("c")[0], f("c_ctx")], axis=-1)
    mod_b = f("mod_b")
    modbT = np.ascontiguousarray(mod_b.reshape(DEPTH, 72, 128).transpose(2, 0, 1))
    ngT = _fm(f("norm_g"))
    conv_kT = _fm(f("conv_k"))
    qkgT = np.ascontiguousarray(np.stack([f("attn_q_g")[0], f("attn_k_g")[0]], axis=-1))
    bg = f("mlstm_b_gate")[0].reshape(-1)
    bgate_rep = np.ascontiguousarray(np.broadcast_to(bg[None, :], (128, 16)))
    mng_rep = np.ascontiguousarray(np.broadcast_to(f("mlstm_norm_g")[0][None, :], (128, D)))
    cos, sin = _rope_tables()
    cmat = _consts()
    shared = {"ctxT": np.ascontiguousarray(ctx.T), "modbT": modbT, "ngT": ngT, "conv_kT": conv_kT, "qkgT": qkgT,
              "bgate_rep": bgate_rep, "mng_rep": mng_rep, "cmat": cmat}
    wfull = {}
    for n, nm, r, c in Builder.WSPEC:
        wfull[n] = f(n).reshape(nm, NCORES, r // NCORES, c)
    mod_w = f("mod_w").reshape(DEPTH, NCORES, 128, 9 * D)
    maps = []
    for r in range(NCORES):
        m = dict(shared)
        sl = slice(r * TPC, (r + 1) * TPC)
        m["xT"] = np.ascontiguousarray(x[sl].T)
        m["cosT"] = np.ascontiguousarray(cos[sl].T)
        m["sinT"] = np.ascontiguousarray(sin[sl].T)
        m["cc_own"] = np.ascontiguousarray(cc[r * 128:(r + 1) * 128])
        m["mod_w"] = np.ascontiguousarray(mod_w[:, r])
        for n in wfull:
            m[n] = np.ascontiguousarray(wfull[n][:, r])
        hs = np.zeros((128, 2, 16), np.float32)
        if r > 0:
            hs[:, 0, (r - 1) * 2 + 1] = 1.0
        if r < NCORES - 1:
            hs[:, 1, (r + 1) * 2 + 0] = 1.0
        m["halo_sel"] = hs
        sm = np.zeros((8, 18, 128), np.float32)
        sv = np.zeros((128, 18), np.float32)
        for d in range(2):
            before = list(range(0, r)) if d == 0 else list(range(NCORES - 1, r, -1))
            for idx, rp in enumerate(before):
                for rpp in before[idx + 1:]:
                    sm[rpp, d * 9 + rp, :] = 1.0
                sv[:, d * 9 + rp] = 1.0
            for rpp in before:
                sm[rpp, d * 9 + 8, :] = 1.0
            sv[:, d * 9 + 8] = 1.0
        m["scan_mask"] = sm
        m["scan_valid"] = sv
        maps.append(m)
    return maps


def run(inp, stop=None):
    key = ("nc", stop)
    if key not in _CACHE:
        _CACHE[key] = Builder(stop).build()
    nc = _CACHE[key]
    maps = make_in_maps(inp)
    res = run_bass_kernel_spmd(nc, maps, core_ids=list(range(NCORES)))
    outs = [np.asarray(r["outT"]) for r in res.results]
    lat = np.concatenate([o.T for o in outs], axis=0)[None]
    dbg = [np.asarray(r["dbgT"]).T for r in res.results]
    return np.ascontiguousarray(lat.astype(np.float32)), dbg


def kernel(**inputs):
    lat, _ = run(inputs, None)
    return lat
```

```python
import numpy as np
import ml_dtypes
from contextlib import ExitStack, contextmanager
import concourse.bass as bass
import concourse.mybir as mybir
from concourse.bass_utils import run_bass_kernel_spmd

F32 = mybir.dt.float32
BF16 = mybir.dt.bfloat16
AF = mybir.ActivationFunctionType
ALU = mybir.AluOpType
AX = mybir.AxisListType

NCORES = 8
D = 1024
SEQ = 16384
TPC = SEQ // NCORES
CTX = 256
FH = 2816
DEPTH = 4
NB = 4
BN = 512
EPS = 1e-6
GRID_W = 64
DH = 128
MH = 4
MDV = 256
MDK = 128
M_KVG = 512 + 1024 + 16
M_IN = M_KVG + 512 + 1024


class SemSlot:
    def __init__(self, name, unit=16):
        self.name = name
        self.unit = unit
        self.groups = []
        self.closed = True
        self.sem = None
        self.ends = None


class View:
    __slots__ = ("tt", "ap", "key")

    def __init__(self, tt, ap, key):
        self.tt = tt
        self.ap = ap
        self.key = key


class _Keyed:
    __slots__ = ("tt", "key")

    def __init__(self, tt, key):
        self.tt = tt
        self.key = key

    def __getitem__(self, idx):
        return View(self.tt, self.tt.h[idx], self.key)


class TT:
    def __init__(self, name, h, slot=None):
        self.name = name
        self.h = h
        self.st = {}
        self.slot = slot

    def __getitem__(self, idx):
        return View(self, self.h[idx], None)

    def K(self, key):
        return _Keyed(self, key)


class Op:
    __slots__ = ("eng", "fn", "cdeps", "ddeps", "sig", "sigval", "dma", "slot", "grp", "idx", "tt")

    def __init__(self, eng, fn):
        self.eng = eng
        self.fn = fn
        self.cdeps = {}
        self.ddeps = set()
        self.sig = False
        self.sigval = 0
        self.dma = False
        self.slot = None
        self.grp = -1
        self.idx = -1
        self.tt = None


ENGS = ("pe", "act", "dve", "pool", "sp")


class Prog:
    def __init__(self, nc):
        self.nc = nc
        self.ops = {e: [] for e in ENGS}
        self.slots = {}
        self.last_compute = {e: None for e in ENGS}
        self.bar_dma = []

    def slot(self, name, unit=16):
        if name not in self.slots:
            self.slots[name] = SemSlot(name, unit)
        return self.slots[name]

    def _dep(self, op, o, kind):
        if o is None or o is op:
            return
        if o.dma:
            if op.dma and o.slot is op.slot and o.grp == op.grp:
                return
            op.ddeps.add((o.slot, o.grp))
            if o.grp == len(o.slot.groups) - 1:
                o.slot.closed = True
            return
        if o.eng == op.eng and op.eng == "pe":
            return
        cur = op.cdeps.get(o.eng)
        if cur is None or o.idx > cur.idx:
            op.cdeps[o.eng] = o

    def _acc(self, op, v, write):
        if not isinstance(v, View):
            return
        st = v.tt.st
        key = None if v.tt.name.startswith("ps") else v.key
        if key is not None:
            keys = [key, "*"]
        else:
            keys = list(st.keys())
        for k in keys:
            ent = st.get(k)
            if ent is None:
                continue
            self._dep(op, ent[0], "W")
            if write:
                for r in ent[1]:
                    self._dep(op, r, "R")
        if write:
            if key is None:
                st.clear()
                st["*"] = [op, []]
            else:
                st[key] = [op, []]
        elif op.fn is not None:
            k = key if key is not None else "*"
            if k not in st:
                st[k] = [None, []]
            st[k][1].append(op)

    def op(self, eng, fn, reads=(), writes=(), dma_out=None):
        op = Op(eng, fn)
        lst = self.ops[eng]
        op.idx = len(lst)
        if dma_out is not None:
            op.dma = True
            slot = dma_out.tt.slot
            assert slot is not None, dma_out.tt.name
            op.slot = slot
            if slot.closed:
                slot.groups.append(0)
                slot.closed = False
                self.bar_dma.append((slot, len(slot.groups) - 1))
            op.grp = len(slot.groups) - 1
            slot.groups[op.grp] += 1
        for v in reads:
            self._acc(op, v, False)
        for v in writes:
            self._acc(op, v, True)
        if dma_out is not None:
            self._acc(op, dma_out, True)
        lst.append(op)
        if fn is not None and not op.dma:
            self.last_compute[eng] = op
        return op

    def barrier(self):
        lasts = dict(self.last_compute)
        keep = [(s, g) for (s, g) in self.bar_dma if s.name.startswith("bnc") or (s.name[:2] == "cg" and s.name[2:].isdigit())]
        dmas = [(s, g) for (s, g) in self.bar_dma if (s, g) not in keep]
        for s, g in dmas:
            if g == len(s.groups) - 1:
                s.closed = True
        self.bar_dma = [(s, g) for (s, g) in keep if g == len(s.groups) - 1 and not s.closed]
        for e in ENGS:
            op = Op(e, None)
            op.idx = len(self.ops[e])
            for e2, o in lasts.items():
                if o is not None and e2 != e:
                    op.cdeps[e2] = o
            op.ddeps = set(dmas)
            self.ops[e].append(op)

    @staticmethod
    def _a(x):
        return x.ap if isinstance(x, View) else x

    def matmul(self, out, lhsT, rhs, start=True, stop=True):
        a = self._a
        return self.op("pe", lambda e: e.matmul(a(out), a(lhsT), a(rhs), start=start, stop=stop),
                       reads=(lhsT, rhs), writes=(out,))

    def transpose(self, out, in_, ident):
        a = self._a
        return self.op("pe", lambda e: e.transpose(a(out), a(in_), a(ident)), reads=(in_, ident), writes=(out,))

    def act(self, out, in_, func, bias=None, scale=1.0, accum=None):
        a = self._a
        rd = [in_]
        kw = {}
        if bias is not None:
            kw["bias"] = a(bias)
            rd.append(bias)
        if isinstance(scale, View):
            rd.append(scale)
        kw["scale"] = a(scale)
        wr = [out]
        if accum is not None:
            kw["accum_out"] = a(accum)
            wr.append(accum)
        return self.op("act", lambda e: e.activation(out=a(out), in_=a(in_), func=func, **kw), reads=rd, writes=wr)

    def tt(self, eng, out, in0, in1, op):
        a = self._a
        return self.op(eng, lambda e: e.tensor_tensor(out=a(out), in0=a(in0), in1=a(in1), op=op),
                       reads=(in0, in1), writes=(out,))

    def ts(self, eng, out, in0, s1, s2, op0, op1=None):
        a = self._a
        rd = [in0] + [s for s in (s1, s2) if isinstance(s, View)]
        if op1 is None:
            return self.op(eng, lambda e: e.tensor_scalar(out=a(out), in0=a(in0), scalar1=a(s1), scalar2=None, op0=op0),
                           reads=rd, writes=(out,))
        return self.op(eng, lambda e: e.tensor_scalar(out=a(out), in0=a(in0), scalar1=a(s1), scalar2=a(s2),
                                                      op0=op0, op1=op1), reads=rd, writes=(out,))

    def stt(self, eng, out, in0, scalar, in1, op0, op1):
        a = self._a
        rd = [in0, in1] + ([scalar] if isinstance(scalar, View) else [])
        return self.op(eng, lambda e: e.scalar_tensor_tensor(out=a(out), in0=a(in0), scalar=a(scalar), in1=a(in1),
                                                             op0=op0, op1=op1), reads=rd, writes=(out,))

    def copy(self, eng, out, in_):
        a = self._a
        if eng == "act":
            return self.op(eng, lambda e: e.activation(out=a(out), in_=a(in_), func=AF.Identity), reads=(in_,), writes=(out,))
        return self.op(eng, lambda e: e.tensor_copy(out=a(out), in_=a(in_)), reads=(in_,), writes=(out,))

    def memset(self, eng, out, val):
        a = self._a
        return self.op(eng, lambda e: e.memset(a(out), val), writes=(out,))

    def recip(self, out, in_):
        a = self._a
        return self.op("dve", lambda e: e.reciprocal(out=a(out), in_=a(in_)), reads=(in_,), writes=(out,))

    def dma(self, eng, out, in_):
        a = self._a
        assert isinstance(out, View)
        return self.op(eng, lambda e: e.dma_start(out=a(out), in_=a(in_)), reads=(in_,), dma_out=out)

    def collective(self, out, in_):
        a = self._a
        assert out.tt.slot.unit == 1
        return self.op("pool", lambda e: e.collective_compute("AllGather", ALU.bypass,
                                                               replica_groups=[list(range(NCORES))],
                                                               ins=[a(in_).opt()], outs=[a(out).opt()]),
                       reads=(in_,), dma_out=out)

    def selfcheck(self):
        for e in ENGS:
            for op in self.ops[e]:
                for o in op.cdeps.values():
                    o.sig = True
        for e in ENGS:
            cnt = 0
            for op in self.ops[e]:
                if op.sig:
                    cnt += 1
                    op.sigval = cnt
        for s in self.slots.values():
            tot = 0
            s.ends = []
            for g in s.groups:
                tot += g
                s.ends.append(s.unit * tot)
        sem = {}
        pc = {e: 0 for e in ENGS}
        progress = True
        while progress:
            progress = False
            for e in ENGS:
                while pc[e] < len(self.ops[e]):
                    op = self.ops[e][pc[e]]
                    ok = all(sem.get(e2, 0) >= o.sigval for e2, o in op.cdeps.items()) and \
                        all(sem.get(s.name, 0) >= s.ends[g] for (s, g) in op.ddeps)
                    if not ok:
                        break
                    if op.fn is not None:
                        if op.dma:
                            sem[op.slot.name] = sem.get(op.slot.name, 0) + op.slot.unit
                        elif op.sig:
                            sem[e] = sem.get(e, 0) + 1
                    pc[e] += 1
                    progress = True
        stuck = {e: (pc[e], len(self.ops[e])) for e in ENGS if pc[e] < len(self.ops[e])}
        for e, (i, n) in stuck.items():
            op = self.ops[e][i]
            print("STUCK", e, i, n, "cdeps", {e2: (o.sigval, sem.get(e2, 0)) for e2, o in op.cdeps.items()},
                  "ddeps", [(s.name, s.ends[g], sem.get(s.name, 0)) for (s, g) in op.ddeps])
        return not stuck

    def emit(self):
        nc = self.nc
        for e in ENGS:
            for op in self.ops[e]:
                for o in op.cdeps.values():
                    o.sig = True
        for e in ENGS:
            cnt = 0
            for op in self.ops[e]:
                if op.sig:
                    cnt += 1
                    op.sigval = cnt
        with ExitStack() as es:
            sems = {e: es.enter_context(nc.semaphore("s_" + e)) for e in ENGS}
            for s in self.slots.values():
                s.sem = es.enter_context(nc.semaphore("d_" + s.name))
                tot = 0
                s.ends = []
                for g in s.groups:
                    tot += g
                    s.ends.append(s.unit * tot)
            block = es.enter_context(nc.Block())

            def mk(e):
                def body(eng):
                    waited = {}
                    for op in self.ops[e]:
                        for e2, o in op.cdeps.items():
                            v = o.sigval
                            if waited.get(e2, 0) < v:
                                eng.wait_ge(sems[e2], v)
                                waited[e2] = v
                        for (s, g) in op.ddeps:
                            v = s.ends[g]
                            if waited.get(s.name, 0) < v:
                                eng.wait_ge(s.sem, v)
                                waited[s.name] = v
                        if op.fn is not None:
                            ins = op.fn(eng)
                            if op.dma:
                                if op.slot.unit == 16:
                                    ins.then_inc(op.slot.sem, 16)
                                else:
                                    ins.then_inc(op.slot.sem)
                            elif op.sig:
                                ins.then_inc(sems[e], 1)
                return body

            block.tensor(mk("pe"))
            block.scalar(mk("act"))
            block.vector(mk("dve"))
            block.gpsimd(mk("pool"))
            block.sync(mk("sp"))


class Scope:
    def __init__(self, P):
        self.P = P
        self.es = ExitStack()

    _uid = [0]

    def sb(self, name, shape, dtype, slot=None):
        Scope._uid[0] += 1
        name = f"sb_{name}_{Scope._uid[0]}"
        h = self.es.enter_context(self.P.nc.sbuf_tensor(name, list(shape), dtype))
        return TT(name, h, self.P.slot(slot) if slot else None)

    def __enter__(self):
        return self

    def __exit__(self, *a):
        self.P.barrier()
        self.es.close()
        return False


class Builder:
    def __init__(self, stop=None):
        self.stop = stop
        self.nc = bass.Bass("TRN2", target_bir_lowering=False, num_devices=NCORES)
        self.P = Prog(self.nc)
        self.ncg = 0

    def din(self, name, shape, dt=F32):
        return self.nc.dram_tensor(name, list(shape), dt, kind="ExternalInput").ap()

    def dram(self, name, shape, dt, slot, unit=16):
        h = self.nc.dram_tensor(name, list(shape), dt).ap()
        return TT(name, h, self.P.slot(slot, unit))

    WSPEC = [
        ("ffn_w13", 8, D, 2 * FH), ("ffn_w2", 8, FH, D), ("conv_w_in", 2, D, 3 * D), ("conv_w_out", 2, D, D),
        ("attn_w_qkv", 1, D, 1536), ("attn_w_o", 1, D, D), ("mlstm_w_in", 1, D, M_IN), ("mlstm_w_o", 1, D, D),
    ]
    WORDER = [("ffn_w13", 0), ("ffn_w2", 0), ("conv_w_in", 0), ("conv_w_out", 0), ("ffn_w13", 1), ("ffn_w2", 1),
              ("ffn_w13", 2), ("ffn_w2", 2), ("attn_w_qkv", 0), ("attn_w_o", 0), ("ffn_w13", 3), ("ffn_w2", 3),
              ("ffn_w13", 4), ("ffn_w2", 4), ("mlstm_w_in", 0), ("mlstm_w_o", 0), ("ffn_w13", 5), ("ffn_w2", 5),
              ("ffn_w13", 6), ("ffn_w2", 6), ("conv_w_in", 1), ("conv_w_out", 1), ("ffn_w13", 7), ("ffn_w2", 7)]

    def gather_weights(self):
        P = self.P
        spec = {n: (nm, r, c) for n, nm, r, c in self.WSPEC}
        shards = {n: self.din(n, [nm, r // NCORES, c]) for n, nm, r, c in self.WSPEC}
        self.W = {}
        pend = []
        for i, (n, m) in enumerate(self.WORDER):
            nm, r, c = spec[n]
            b = self.dram(f"wb_{n}_{m}", [r // NCORES, c], F32, "bnc0" if i < 4 else "bnc1")
            g = self.dram(f"wg_{n}_{m}", [r, c], F32, f"cg{i}", unit=1)
            self.W[(n, m)] = g
            P.dma("sp", b[:, :], shards[n][m])
            pend.append((g, b))
        for g, b in pend:
            P.collective(g[:, :], b[:, :])

    def build(self):
        nc, P = self.nc, self.P
        I = {}
        I["xT"] = self.din("xT", [D, TPC])
        I["ctxT"] = self.din("ctxT", [D, CTX])
        I["cc_own"] = self.din("cc_own", [128, 2])
        I["mod_w"] = self.din("mod_w", [DEPTH, 128, 9 * D])
        I["modbT"] = self.din("modbT", [128, DEPTH, 72])
        I["ngT"] = self.din("ngT", [128, DEPTH, 3, 8])
        I["conv_kT"] = self.din("conv_kT", [128, 2, 3, 8])
        I["qkgT"] = self.din("qkgT", [128, 2])
        I["bgate_rep"] = self.din("bgate_rep", [128, 16])
        I["mng_rep"] = self.din("mng_rep", [128, D])
        I["cosT"] = self.din("cosT", [128, TPC])
        I["sinT"] = self.din("sinT", [128, TPC])
        I["cmat"] = self.din("cmat", [128, 6, 128])
        I["halo_sel"] = self.din("halo_sel", [128, 2, 16])
        I["scan_mask"] = self.din("scan_mask", [8, 18, 128])
        I["scan_valid"] = self.din("scan_valid", [128, 18])
        self.I = I
        self.outT = nc.dram_tensor("outT", [D, TPC], F32, kind="ExternalOutput").ap()
        self.dbgT = nc.dram_tensor("dbgT", [D, CTX], F32, kind="ExternalOutput").ap()

        with Scope(P) as G:
            self.G = G
            self.lat = [G.sb(f"lat{b}", [128, 8, BN], F32, slot=f"lat{b}") for b in range(NB)]
            self.cx = G.sb("cx", [128, 8, CTX], F32, slot="cx")
            self.cmat = G.sb("cmat", [128, 6, 128], F32, slot="cst")
            self.small = G.sb("small", [128, 512], F32, slot="cst2")
            self.onesD = G.sb("onesD", [128, 128], BF16)
            self.onesH = G.sb("onesH", [128, 128], BF16)
            self.ones1 = G.sb("ones1", [128, 128], BF16)
            self.onesF = G.sb("onesF", [128, 128], F32)
            self.identB = G.sb("identB", [128, 128], BF16)
            self.modsum = G.sb("modsum", [128, DEPTH * 144], F32)
            self.modT = G.sb("modT", [128, 72, 2], F32)
            self.dm = G.sb("dm", [128, 3, 3, 8, 2], F32)
            self.ps = []
            for i in range(6):
                h = G.es.enter_context(nc.psum_tensor(f"ps{i}", [128, 512], F32))
                self.ps.append(TT(f"ps{i}", h))
            hb = G.es.enter_context(nc.psum_tensor("psb", [128, 1024], BF16))
            self.psb = TT("psb", hb)
            hb2 = G.es.enter_context(nc.psum_tensor("psb2", [128, 1024], BF16))
            self.psb2 = TT("psb2", hb2)
            self.psi = 0

            self.phase0()
            for l in range(DEPTH):
                if self.stop in ("g", "p0"):
                    break
                if self.run_layer(l):
                    break
            self.finish()
        assert P.selfcheck(), "deadlock in recorded program"
        P.emit()
        return nc

    def psum(self):
        t = self.ps[self.psi % 6]
        self.psi += 1
        return t

    SM_NG = 0
    SM_MODB = 96
    SM_CK = 176
    SM_QKG = 224
    SM_BG = 232
    SM_HALO = 256
    SM_VALID = 288
    SM_CC = 320

    def phase0(self):
        P, I = self.P, self.I
        self.gather_weights()
        for b in range(NB):
            P.dma("sp", self.lat[b][:, :, :],
                  I["xT"].rearrange("(kc p) t -> p kc t", p=128)[:, :, b * BN:(b + 1) * BN])
        P.dma("sp", self.cx[:, :, :], I["ctxT"].rearrange("(kc p) t -> p kc t", p=128))
        P.dma("sp", self.cmat[:, :, :], I["cmat"])
        sm = self.small
        P.dma("sp", sm[:, self.SM_NG:self.SM_NG + 96], I["ngT"].rearrange("p a b c -> p (a b c)"))
        P.dma("sp", sm[:, self.SM_CK:self.SM_CK + 48], I["conv_kT"].rearrange("p a b c -> p (a b c)"))
        P.dma("sp", sm[:, self.SM_QKG:self.SM_QKG + 2], I["qkgT"])
        P.dma("sp", sm[:, self.SM_BG:self.SM_BG + 16], I["bgate_rep"])
        P.dma("sp", sm[:, self.SM_HALO:self.SM_HALO + 32], I["halo_sel"].rearrange("p a b -> p (a b)"))
        P.dma("sp", sm[:, self.SM_VALID:self.SM_VALID + 18], I["scan_valid"])
        P.dma("sp", sm[:, self.SM_CC:self.SM_CC + 2], I["cc_own"])
        P.memset("dve", self.onesD[:, :], 1.0 / 1024.0)
        P.memset("dve", self.onesH[:, :], 1.0 / 128.0)
        P.memset("dve", self.ones1[:, :], 1.0)
        P.memset("dve", self.onesF[:, :], 1.0)
        P.copy("dve", self.identB[:, :], self.cmat[:, 0, :])
        if self.stop == "g":
            return
        with Scope(P) as S:
            sc = S.sb("sc", [128, 2], F32)
            P.act(sc[:, :], sm[:, self.SM_CC:self.SM_CC + 2], AF.Silu)
            mw = [S.sb(f"mw{i}", [128, 9 * D], F32, slot=f"mw{i}") for i in range(2)]
            part = S.sb("mpart", [128, DEPTH * 144], F32)
            mg = S.sb("mg", [128, NCORES, DEPTH * 144], F32, slot="ld0")
            mb = self.dram("mod_b_", [128, DEPTH * 144], F32, "dr0")
            mgd = self.dram("mod_g_", [NCORES * 128, DEPTH * 144], F32, "cgm", unit=1)
            for l in range(DEPTH):
                w = mw[l % 2]
                P.dma("sp", w[:, :], I["mod_w"][l])
                pm = self.psum()
                for j in range(72):
                    P.matmul(pm.K(j)[:, 2 * j:2 * j + 2], w[:, j * 128:(j + 1) * 128], sc[:, :])
                P.copy("dve", part[:, l * 144:(l + 1) * 144], pm[:, 0:144])
            P.dma("sp", mb[:, :], part[:, :])
            P.collective(mgd[:, :], mb[:, :])
            P.dma("sp", mg[:, :, :], View(mgd, mgd.h.rearrange("(r p) f -> p r f", p=128), None))
            P.op("dve", lambda e: e.tensor_reduce(out=self.modsum.h[:, :], in_=mg.h[:, :, :].rearrange("p r f -> p f r"),
                                                  axis=AX.X, op=ALU.add),
                 reads=(mg[:, :, :],), writes=(self.modsum[:, :],))

    def ctx_flags(self, l):
        rd = [True, True, True, False][l]
        out = [True, True, False, False][l]
        return rd, out

    def check_stop(self, tag):
        return self.stop == tag

    def run_layer(self, l):
        ctx_rd, ctx_out = self.ctx_flags(l)
        if self.stop in ("only_attn", "only_mlstm"):
            want = 1 if self.stop == "only_attn" else 2
            if l != want:
                return False
            self.mod_phase(l)
            if want == 1:
                self.attn_mixer(l)
            else:
                self.mlstm_mixer(l)
            return True
        self.mod_phase(l)
        if self.check_stop(f"{l}m"):
            P = self.P
            P.copy("dve", self.cx[:, 0, 0:144], View(self.dm, self.dm.h[:, :, :, :, :].rearrange("p a b c d -> p (a b c d)"), None))
            return True
        if self.check_stop(f"{l}n"):
            with Scope(self.P) as S:
                hTs = self.norm_phase(S, 0, ctx_rd)
                for kc in range(8):
                    self.P.copy("dve", self.cx[:, kc, :], hTs[4][:, kc, :])
                    self.P.copy("dve", self.lat[1][:, kc, :], hTs[1][:, kc, :])
            return True
        self.ffn_sub(l, 0, ctx_rd)
        if self.check_stop(f"{l}a"):
            return True
        kind = l % 3
        if kind == 0:
            self.conv_mixer(l, l // 3, ctx_out)
        elif kind == 1:
            self.attn_mixer(l)
        else:
            self.mlstm_mixer(l)
        if self.check_stop(f"{l}b") or self.stop == "0h":
            return True
        self.ffn_sub(l, 2, ctx_out)
        if self.check_stop(f"{l}c"):
            return True
        return False

    def mod_phase(self, l):
        P, I = self.P, self.I
        sm = self.small
        P.dma("sp", sm[:, self.SM_MODB:self.SM_MODB + 72], I["modbT"][:, l, :])
        for col in range(2):
            pv = View(self.modsum, self.modsum.h[:, l * 144:(l + 1) * 144].rearrange("p (j c) -> p j c", c=2)[:, :, col], None)
            P.tt("dve", self.modT[:, :, col], pv, sm[:, self.SM_MODB:self.SM_MODB + 72], ALU.add)
        for s in range(3):
            for col in range(2):
                def mv(kindi):
                    j0 = (s * 3 + kindi) * 8
                    return self.modT[:, j0:j0 + 8, col]
                o = self.SM_NG + (l * 3 + s) * 8
                ng = sm[:, o:o + 8]
                P.stt("dve", self.dm[:, s, 0, :, col], mv(1), 1.0, ng, ALU.add, ALU.mult)
                P.copy("dve", self.dm[:, s, 1, :, col], mv(0))
                P.ts("dve", self.dm[:, s, 2, :, col], mv(2), 0.5 if s != 1 else 1.0, None, ALU.mult)

    def rsqrt_eps(self, out, in_):
        P = self.P
        P.ts("dve", out, in_, EPS, None, ALU.add)
        P.act(out, out, AF.Ln)
        P.act(out, out, AF.Exp, scale=-0.5)

    def dmv(self, s, k, fc, col):
        return self.dm[:, s, k, fc, col:col + 1]

    def blocks(self, with_ctx):
        bl = [(self.lat[b], BN, 0) for b in range(NB)]
        if with_ctx:
            bl.append((self.cx, CTX, 1))
        return bl

    def norm_phase(self, S, s, with_ctx, name="hT"):
        P = self.P
        bl = self.blocks(with_ctx)
        hTs = [S.sb(f"{name}{i}", [128, 8, n], BF16) for i, (_, n, _) in enumerate(bl)]
        with Scope(P) as S2:
            sq = [S2.sb(f"sq{i}", [128, 8, BN], BF16) for i in range(2)]
            rstd = [S2.sb(f"rstd{i}", [128, BN], F32) for i in range(2)]
            tmp = [S2.sb(f"ntmp{i}", [128, BN], F32) for i in range(3)]
            ti = 0
            for i, (src, n, col) in enumerate(bl):
                q = sq[i % 2]
                P.act(q[:, :, 0:n], src[:, :, :], AF.Square)
                pm = self.psum()
                for kc in range(8):
                    P.matmul(pm[:, 0:n], self.onesD[:, :], q[:, kc, 0:n], start=(kc == 0), stop=(kc == 7))
                r = rstd[i % 2]
                self.rsqrt_eps(r[:, 0:n], pm[:, 0:n])
                for kc in range(8):
                    t = tmp[ti % 3]
                    ti += 1
                    P.stt("dve", t[:, 0:n], src[:, kc, :], self.dmv(s, 0, kc, col), r[:, 0:n], ALU.mult, ALU.mult)
                    P.act(hTs[i].K(kc)[:, kc, :], t[:, 0:n], AF.Identity, bias=self.dmv(s, 1, kc, col))
        return hTs

    def wview(self, name, m, pat="(kc p) c -> p kc c"):
        g = self.W[(name, m)]
        return g, g.h.rearrange(pat, p=128)

    def ffn_sub(self, l, s, with_ctx):
        P = self.P
        fi = l * 2 + (0 if s == 0 else 1)
        g13, w13 = self.wview("ffn_w13", fi)
        g2, w2 = self.wview("ffn_w2", fi, "(hc p) o -> p hc o")
        bl = self.blocks(with_ctx)
        with Scope(P) as S:
            hTs = self.norm_phase(S, s, with_ctx)
            wa = [S.sb(f"wa{i}", [128, 8, 2, 512], BF16, slot=f"w{i}") for i in range(3)]
            wbb = [S.sb(f"wb{i}", [128, 4, D], BF16, slot=f"v{i}") for i in range(3)]
            gT = [S.sb(f"gT{i}", [128, 4, BN], BF16) for i in range(2)]
            sg = [S.sb(f"sg{i}", [128, BN], F32) for i in range(2)]
            slices = [(0, 4), (4, 4), (8, 4), (12, 4), (16, 4), (20, 2)]
            gi = 0
            si = 0
            for sl, (c0, nch) in enumerate(slices):
                a = wa[sl % 3]
                b2 = wbb[sl % 3]
                nco = nch * 128
                for kc in range(8):
                    P.dma("pool", a[:, kc, 0, 0:nco], View(g13, w13[:, kc, c0 * 128:c0 * 128 + nco], None))
                    P.dma("pool", a[:, kc, 1, 0:nco], View(g13, w13[:, kc, FH + c0 * 128:FH + c0 * 128 + nco], None))
                for hc in range(nch):
                    P.dma("pool", b2[:, hc, :], View(g2, w2[:, c0 + hc, :], None))
                for bi, (dst, n, col) in enumerate(bl):
                    g = gT[gi % 2]
                    gi += 1
                    for j in range(nch):
                        pg = self.psum()
                        pu = self.psum()
                        for kc in range(8):
                            P.matmul(pg[:, 0:n], a[:, kc, 0, j * 128:(j + 1) * 128], hTs[bi][:, kc, :],
                                     start=(kc == 0), stop=(kc == 7))
                        for kc in range(8):
                            P.matmul(pu[:, 0:n], a[:, kc, 1, j * 128:(j + 1) * 128], hTs[bi][:, kc, :],
                                     start=(kc == 0), stop=(kc == 7))
                        sgt = sg[si % 2]
                        si += 1
                        P.act(sgt[:, 0:n], pg[:, 0:n], AF.Silu)
                        P.tt("dve", g.K(j)[:, j, 0:n], pu[:, 0:n], sgt[:, 0:n], ALU.mult)
                    for oc in range(8):
                        po = self.psum()
                        for j in range(nch):
                            P.matmul(po[:, 0:n], b2[:, j, oc * 128:(oc + 1) * 128], g.K(j)[:, j, 0:n],
                                     start=(j == 0), stop=(j == nch - 1))
                        P.stt("dve", dst.K(oc)[:, oc, :], po[:, 0:n], self.dmv(s, 2, oc, col), dst.K(oc)[:, oc, :],
                              ALU.mult, ALU.add)

    def out_proj(self, S, wname, yTs, bl, wslot="w0"):
        P = self.P
        gw, wv = self.wview(wname, 0 if not isinstance(wname, tuple) else 0)
        wo = S.sb("wo", [128, 8, D], BF16, slot=wslot)
        P.dma("pool", wo[:, :, :], View(gw, wv, None))
        for bi, (dst, n, col) in enumerate(bl):
            for oc in range(8):
                po = self.psum()
                for kc in range(8):
                    P.matmul(po[:, 0:n], wo[:, kc, oc * 128:(oc + 1) * 128], yTs[bi][:, kc, :],
                             start=(kc == 0), stop=(kc == 7))
                P.stt("dve", dst.K(oc)[:, oc, :], po[:, 0:n], self.dmv(1, 2, oc, col), dst.K(oc)[:, oc, :],
                      ALU.mult, ALU.add)

    def conv_mixer(self, l, j, with_ctx):
        P = self.P
        sm = self.small
        gin, win = self.wview("conv_w_in", j)
        bl = self.blocks(with_ctx)

        def ck(tap, fc):
            o = self.SM_CK + (j * 3 + tap) * 8 + fc
            return sm[:, o:o + 1]
        with Scope(P) as S:
            hTs = self.norm_phase(S, 1, with_ctx)
            yTs = [S.sb(f"yT{i}", [128, 8, n], BF16) for i, (_, n, _) in enumerate(bl)]
            halo = S.sb("halo", [128, 2, 8], F32)
            with Scope(P) as S1:
                hb = S1.sb("hb", [128, 8, 2], BF16)
                P.copy("dve", hb[:, :, 0], hTs[0][:, :, 0])
                P.copy("dve", hb[:, :, 1], hTs[NB - 1][:, :, BN - 1])
                wp = [S1.sb(f"wp{i}", [128, 8, 2, 128], BF16, slot=f"w{i}") for i in range(2)]
                cxb = S1.sb("cxb", [128, 8, 2], F32)
                csb = S1.sb("csb", [128, 2], F32)
                hg = S1.sb("hg", [128, NCORES, 16], F32, slot="ld0")
                t16 = S1.sb("t16", [128, 16], F32)
                for oc in range(8):
                    w = wp[oc % 2]
                    for k in range(2):
                        c0 = (k + 1) * D + oc * 128
                        for kc in range(8):
                            P.dma("pool", w[:, kc, k, :], View(gin, win[:, kc, c0:c0 + 128], None))
                    pm = self.psum()
                    for k in range(2):
                        for kc in range(8):
                            P.matmul(pm.K(k)[:, 2 * k:2 * k + 2], w[:, kc, k, :], hb[:, kc, :], start=(kc == 0), stop=(kc == 7))
                    P.copy("act", csb[:, :], pm.K(0)[:, 0:2])
                    P.tt("dve", cxb[:, oc, :], pm.K(1)[:, 2:4], csb[:, :], ALU.mult)
                hbd = self.dram(f"halo_b{l}", [128, 16], F32, "dr0")
                hgd = self.dram(f"halo_g{l}", [NCORES * 128, 16], F32, f"cgh{l}", unit=1)
                P.dma("sp", hbd[:, :], cxb[:, :, :].ap.rearrange("p a b -> p (a b)") if False else
                      View(cxb, cxb.h[:, :, :].rearrange("p a b -> p (a b)"), None))
                P.collective(hgd[:, :], hbd[:, :])
                P.dma("sp", hg[:, :, :], View(hgd, hgd.h.rearrange("(r p) f -> p r f", p=128), None))
                for side in range(2):
                    sel = View(sm, sm.h[:, self.SM_HALO + side * 16:self.SM_HALO + side * 16 + 16]
                               .rearrange("p (r t) -> p r t", t=2), None)
                    for oc in range(8):
                        P.tt("dve", View(t16, t16.h[:, :].rearrange("p (r t) -> p r t", t=2), None),
                             hg[:, :, oc * 2:oc * 2 + 2], sel, ALU.mult)
                        P.op("dve", (lambda e, side=side, oc=oc: e.reduce_sum(out=halo.h[:, side, oc:oc + 1], in_=t16.h[:, :],
                                                                                axis=AX.X)),
                             reads=(t16[:, :],), writes=(halo.K((side, oc))[:, side, oc:oc + 1],))
            if self.stop == "0h":
                P.copy("dve", self.cx[:, 0, 0:16], View(halo, halo.h[:, :, :].rearrange("p a b -> p (a b)"), None))
                return
            with Scope(P) as S2:
                wp = [S2.sb(f"wq{i}", [128, 8, 3, 128], BF16, slot=f"w{i}") for i in range(2)]
                cxo = [S2.sb(f"cxo{i}", [128, TPC + 2], F32) for i in range(2)]
                cxc = [S2.sb(f"cxc{i}", [128, CTX + 2], F32) for i in range(2)] if with_ctx else None
                csb = [S2.sb(f"csb{i}", [128, BN], F32) for i in range(2)]
                cv = [S2.sb(f"cv{i}", [128, BN], F32) for i in range(2)]
                ci = 0
                for oc in range(8):
                    w = wp[oc % 2]
                    for k in range(3):
                        c0 = k * D + oc * 128
                        for kc in range(8):
                            P.dma("pool", w[:, kc, k, :], View(gin, win[:, kc, c0:c0 + 128], None))
                    co = cxo[oc % 2]
                    P.copy("act", co.K("hl")[:, 0:1], halo[:, 0, oc:oc + 1])
                    P.copy("act", co.K("hr")[:, TPC + 1:TPC + 2], halo[:, 1, oc:oc + 1])
                    srcs = []
                    for bi, (dst, n, col) in enumerate(bl):
                        if col == 0:
                            tgt, off = co, 1 + bi * BN
                        else:
                            tgt, off = cxc[oc % 2], 1
                            P.memset("pool", tgt.K("hl")[:, 0:1], 0.0)
                            P.memset("pool", tgt.K("hr")[:, CTX + 1:CTX + 2], 0.0)
                        pc = self.psum()
                        px = self.psum()
                        for kc in range(8):
                            P.matmul(pc[:, 0:n], w[:, kc, 1, :], hTs[bi][:, kc, :], start=(kc == 0), stop=(kc == 7))
                        for kc in range(8):
                            P.matmul(px[:, 0:n], w[:, kc, 2, :], hTs[bi][:, kc, :], start=(kc == 0), stop=(kc == 7))
                        cs = csb[ci % 2]
                        ci += 1
                        P.copy("act", cs[:, 0:n], pc[:, 0:n])
                        P.tt("dve", tgt.K(bi)[:, off:off + n], px[:, 0:n], cs[:, 0:n], ALU.mult)
                        srcs.append((tgt, off))
                    for bi, (dst, n, col) in enumerate(bl):
                        tgt, off = srcs[bi]
                        c = cv[ci % 2]
                        ci += 1
                        P.ts("pool", c[:, 0:n], tgt[:, off - 1:off - 1 + n], ck(0, oc), None, ALU.mult)
                        P.stt("dve", c[:, 0:n], tgt[:, off:off + n], ck(1, oc), c[:, 0:n], ALU.mult, ALU.add)
                        P.stt("dve", c[:, 0:n], tgt[:, off + 1:off + 1 + n], ck(2, oc), c[:, 0:n], ALU.mult, ALU.add)
                        pb = self.psum()
                        for kc in range(8):
                            P.matmul(pb[:, 0:n], w[:, kc, 0, :], hTs[bi][:, kc, :], start=(kc == 0), stop=(kc == 7))
                        P.tt("dve", yTs[bi].K(oc)[:, oc, :], pb[:, 0:n], c[:, 0:n], ALU.mult)
            with Scope(P) as S3:
                wo = self.load_wo(S3, "conv_w_out", j)
                self.proj_out(wo, yTs, bl)

    def proj_out(self, wo, yTs, bl, rhsf=None):
        P = self.P
        for bi, (dst, n, col) in enumerate(bl):
            for oc in range(8):
                po = self.psum()
                for kc in range(8):
                    rhs = yTs[bi][:, kc, :] if rhsf is None else rhsf(bi, kc)
                    P.matmul(po[:, 0:n], wo[:, kc, oc * 128:(oc + 1) * 128], rhs,
                             start=(kc == 0), stop=(kc == 7))
                P.stt("dve", dst.K(oc)[:, oc, :], po[:, 0:n], self.dmv(1, 2, oc, col), dst.K(oc)[:, oc, :],
                      ALU.mult, ALU.add)

    def load_wo(self, S, name, m=0):
        gw, wv = self.wview(name, m)
        wo = S.sb("wo", [128, 8, D], BF16, slot="w0")
        for kc in range(8):
            self.P.dma("pool", wo[:, kc, :], View(gw, wv[:, kc, :], None))
        return wo

    def attn_mixer(self, l):
        P, I = self.P, self.I
        sm = self.small
        gq, wqkv = self.wview("attn_w_qkv", 0)
        bl = self.blocks(True)
        SCALE = DH ** -0.5
        with Scope(P) as S:
            qT = S.sb("qT", [128, 8, TPC], BF16)
            qcT = S.sb("qcT", [128, 8, CTX], BF16)
            kcT = S.sb("kcT", [128, 2, CTX], BF16)
            vc = S.sb("vc", [128, 2, 256], BF16)
            kb = self.dram("att_kb", [2 * 128, TPC // 2], F32, "dr0")
            vb = self.dram("att_vb", [TPC, 128], F32, "dr1")
            kg = self.dram("att_kg", [NCORES * 256, TPC // 2], F32, "cgk", unit=1)
            vg = self.dram("att_vg", [NCORES * TPC, 128], F32, "cgv", unit=1)
            kbf, vbf, kgf, vgf = kb.h, vb.h, kg.h, vg.h
            for t_ in (kb, vb, kg, vg):
                t_.h = t_.h.bitcast(BF16)
            with Scope(P) as S1:
                hTs = self.norm_phase(S1, 1, True)
                wq = [S1.sb(f"wq{i}", [128, 8, 512], BF16, slot=f"w{i}") for i in range(2)]
                cs = [S1.sb(f"cs{i}", [128, 2, BN], F32, slot=f"ld{i}") for i in range(2)]
                sq = [S1.sb(f"sq{i}", [128, BN], BF16) for i in range(2)]
                rs = [S1.sb(f"rs{i}", [128, BN], F32) for i in range(2)]
                qn = [S1.sb(f"qn{i}", [128, BN], F32) for i in range(2)]
                t2 = [S1.sb(f"t2{i}", [128, BN], F32) for i in range(2)]
                kst = [S1.sb(f"kst{i}", [128, BN], BF16) for i in range(2)]
                vst = [S1.sb(f"vst{i}", [128, 256], BF16) for i in range(2)]
                ui = 0
                ci = 0
                vi = 0
                for pc in range(3):
                    w = wq[pc % 2]
                    for kc in range(8):
                        P.dma("pool", w[:, kc, :], View(gq, wqkv[:, kc, pc * 512:(pc + 1) * 512], None))
                    for bi, (src, n, col) in enumerate(bl):
                        c_ = None
                        if col == 0:
                            c_ = cs[ci % 2]
                            ci += 1
                            P.dma("sp", c_[:, 0, :], I["cosT"][:, bi * BN:(bi + 1) * BN])
                            P.dma("sp", c_[:, 1, :], I["sinT"][:, bi * BN:(bi + 1) * BN])
                        nheads = 4 if pc < 2 else 2
                        for hh in range(nheads):
                            u = ui % 2
                            ui += 1
                            pq = self.psum()
                            for kc in range(8):
                                P.matmul(pq[:, 0:n], w[:, kc, hh * 128:(hh + 1) * 128], hTs[bi][:, kc, :],
                                         start=(kc == 0), stop=(kc == 7))
                            P.act(sq[u][:, 0:n], pq[:, 0:n], AF.Square)
                            pm = self.psum()
                            P.matmul(pm[:, 0:n], self.onesH[:, :], sq[u][:, 0:n])
                            self.rsqrt_eps(rs[u][:, 0:n], pm[:, 0:n])
                            go = self.SM_QKG + (0 if pc < 2 else 1)
                            P.stt("dve", qn[u][:, 0:n], pq[:, 0:n], sm[:, go:go + 1], rs[u][:, 0:n], ALU.mult, ALU.mult)
                            if col == 0:
                                pr = self.psum()
                                P.matmul(pr[:, 0:n], self.cmat[:, 1, :], qn[u][:, 0:n])
                                P.tt("dve", t2[u][:, 0:n], pr[:, 0:n], c_[:, 1, :], ALU.mult)
                                P.tt("pool", qn[u][:, 0:n], qn[u][:, 0:n], c_[:, 0, :], ALU.mult)
                                if pc < 2:
                                    h8 = pc * 4 + hh
                                    P.tt("pool", qT.K((h8, bi))[:, h8, bi * BN:(bi + 1) * BN], qn[u][:, 0:n], t2[u][:, 0:n], ALU.add)
                                else:
                                    P.tt("pool", kst[u][:, 0:n], qn[u][:, 0:n], t2[u][:, 0:n], ALU.add)
                                    P.dma("sp", kb.K((hh, bi))[hh * 128:(hh + 1) * 128, bi * BN:(bi + 1) * BN], kst[u][:, 0:n])
                            else:
                                if pc < 2:
                                    P.copy("pool", qcT.K(pc * 4 + hh)[:, pc * 4 + hh, :], qn[u][:, 0:n])
                                else:
                                    P.copy("pool", kcT.K(hh)[:, hh, :], qn[u][:, 0:n])
                        if pc == 2:
                            for st in range(n // 128):
                                pv = self.psum()
                                for kc in range(8):
                                    P.matmul(pv[:, 0:256], hTs[bi][:, kc, st * 128:(st + 1) * 128], w[:, kc, 256:512],
                                             start=(kc == 0), stop=(kc == 7))
                                if col == 0:
                                    vs = vst[vi % 2]
                                    vi += 1
                                    P.copy("act", vs[:, :], pv[:, 0:256])
                                    r0 = (bi * 4 + st) * 128
                                    P.dma("sp", vb.K(r0)[r0:r0 + 128, :], vs[:, :])
                                else:
                                    P.copy("act", vc.K(st)[:, st, :], pv[:, 0:256])
            P.collective(View(kg, kgf, None), View(kb, kbf, None))
            P.collective(View(vg, vgf, None), View(vb, vbf, None))
            with Scope(P) as S2:
                Kall = S2.sb("Kall", [128, CTX + SEQ], BF16, slot="ld0")
                Vall = S2.sb("Vall", [128, 130, 128], BF16, slot="ld1")
                pT = [S2.sb(f"pT{i}", [128, 512], BF16) for i in range(4)]
                rd = [S2.sb(f"rd{i}", [128, 512], F32) for i in range(2)]
                vgv = vg.h.rearrange("(c p) f -> p c f", p=128)
                for kvh in range(2):
                    P.copy("pool", Kall.K("c")[:, 0:CTX], kcT[:, kvh, :])
                    for r in range(NCORES):
                        P.dma("sp", Kall.K(r)[:, CTX + r * TPC:CTX + (r + 1) * TPC],
                              View(kg, kg.h[r * 256 + kvh * 128:r * 256 + (kvh + 1) * 128, :], None))
                    P.copy("pool", Vall.K("c")[:, 0:2, :], vc[:, :, kvh * 128:(kvh + 1) * 128])
                    for c0 in range(0, 128, 8):
                        P.dma("sp", Vall.K(c0)[:, 2 + c0:2 + c0 + 8, :],
                              View(vg, vgv[:, c0:c0 + 8, kvh * 128:(kvh + 1) * 128], None))
                        P.op("sp", None, reads=(Vall.K(c0)[:, 2 + c0:2 + c0 + 8, :],))
                    for qb in range(16 + 2):
                        if qb < 16:
                            qv = qT.K(("o", kvh, qb))[:, kvh * 4:(kvh + 1) * 4, qb * 128:(qb + 1) * 128]
                            nkc = 130
                        else:
                            c = qb - 16
                            qv = qcT.K(("o", kvh, c))[:, kvh * 4:(kvh + 1) * 4, c * 128:(c + 1) * 128]
                            nkc = 2
                        qr = View(qv.tt, qv.ap, None)
                        po = self.ps[qb % 2]
                        pd = self.ps[2 + qb % 2]
                        for kc in range(nkc):
                            ps_ = self.ps[4 + kc % 2]
                            P.matmul(View(ps_, ps_.h[:, :].rearrange("p (a b) -> p a b", a=4), None),
                                     Kall[:, kc * 128:(kc + 1) * 128], qr)
                            p = pT[kc % 4]
                            P.act(p[:, :], ps_[:, :], AF.Exp, scale=SCALE)
                            P.matmul(po[:, :], Vall[:, kc, :], p[:, :], start=(kc == 0), stop=(kc == nkc - 1))
                            P.matmul(pd[:, :], self.ones1[:, :], p[:, :], start=(kc == 0), stop=(kc == nkc - 1))
                        r_ = rd[qb % 2]
                        P.copy("act", r_[:, :], pd[:, :])
                        P.recip(r_[:, :], r_[:, :])
                        P.tt("dve", qv, View(po, po.h[:, :].rearrange("p (a b) -> p a b", a=4), None),
                             View(r_, r_.h[:, :].rearrange("p (a b) -> p a b", a=4), None), ALU.mult)
            with Scope(P) as S3:
                wo = self.load_wo(S3, "attn_w_o")

                def rhsf(bi, kc):
                    if bi < NB:
                        return qT[:, kc, bi * BN:(bi + 1) * BN]
                    return qcT[:, kc, :]
                self.proj_out(wo, None, bl, rhsf)

    def mlstm_mixer(self, l):
        P, I = self.P, self.I
        sm = self.small
        gin, win = self.wview("mlstm_w_in", 0)
        NCH = 18
        KS = MDK ** -0.5
        bl = self.blocks(True)

        def chunk_src(c):
            if c < 2:
                return NB, c * 128
            return (c - 2) // 4, ((c - 2) % 4) * 128
        with Scope(P) as S:
            kT = S.sb("kT", [128, 4, NCH * 128], BF16)
            qT = S.sb("qT", [128, 4, TPC], BF16)
            Gt = S.sb("Gt", [128, NCH, 16], F32)
            vd = self.dram("ml_v", [NCH * 128, 4 * 257], BF16, "dr0")
            sod = self.dram("ml_so", [TPC, D], BF16, "dr1")
            hfd = self.dram("ml_hf", [TPC, D], F32, "dr2")
            with Scope(P) as S1:
                hTs = self.norm_phase(S1, 1, True)
                wp = [S1.sb(f"wp{i}", [128, 8, 512], BF16, slot=f"w{i}") for i in range(2)]
                wg = S1.sb("wg", [128, 8, 16], BF16, slot="v0")
                vt = [S1.sb(f"vt{i}", [128, 2, 257], BF16) for i in range(2)]
                sot = [S1.sb(f"sot{i}", [128, 512], BF16) for i in range(2)]
                for t in vt:
                    P.memset("pool", t[:, :, 256:257], 1.0)
                for kc in range(8):
                    P.dma("pool", wg[:, kc, :], View(gin, win[:, kc, 1536:1552], None))
                pieces = [("k", 0), ("q", 1552), ("v0", 512), ("v1", 1024), ("o0", 2064), ("o1", 2576)]
                ti = 0
                for pi, (kind, c0) in enumerate(pieces):
                    w = wp[pi % 2]
                    for kc in range(8):
                        P.dma("pool", w[:, kc, :], View(gin, win[:, kc, c0:c0 + 512], None))
                    if kind in ("k", "q"):
                        for bi, (src, n, col) in enumerate(bl):
                            if kind == "q" and col == 1:
                                continue
                            for hh in range(4):
                                pk = self.psum()
                                for kc in range(8):
                                    P.matmul(pk[:, 0:n], w[:, kc, hh * 128:(hh + 1) * 128], hTs[bi][:, kc, :],
                                             start=(kc == 0), stop=(kc == 7))
                                if kind == "k":
                                    off = 0 if col == 1 else 256 + bi * BN
                                    P.act(kT.K((hh, bi))[:, hh, off:off + n], pk[:, 0:n], AF.Identity, scale=KS)
                                else:
                                    P.copy("act", qT.K((hh, bi))[:, hh, bi * BN:(bi + 1) * BN], pk[:, 0:n])
                    else:
                        for c in range(NCH):
                            if kind[0] == "o" and c < 2:
                                continue
                            bi, off = chunk_src(c)
                            pv = self.psum()
                            for kc in range(8):
                                P.matmul(pv[:, :], hTs[bi][:, kc, off:off + 128], w[:, kc, :], start=(kc == 0), stop=(kc == 7))
                            half = int(kind[1])
                            if kind[0] == "v":
                                t = vt[ti % 2]
                                ti += 1
                                P.copy("act", t[:, :, 0:256], View(pv, pv.h[:, :].rearrange("p (a b) -> p a b", a=2), None))
                                P.dma("sp", vd.K((c, half))[c * 128:(c + 1) * 128, half * 514:(half + 1) * 514],
                                      View(t, t.h[:, :, :].rearrange("p a b -> p (a b)"), None))
                                if half == 0:
                                    pg = self.psum()
                                    for kc in range(8):
                                        P.matmul(pg[:, 0:16], hTs[bi][:, kc, off:off + 128], wg[:, kc, :],
                                                 start=(kc == 0), stop=(kc == 7))
                                    P.tt("dve", Gt.K(c)[:, c, :], pg[:, 0:16], sm[:, self.SM_BG:self.SM_BG + 16], ALU.add)
                            else:
                                t = sot[ti % 2]
                                ti += 1
                                P.act(t[:, :], pv[:, :], AF.Sigmoid)
                                r0 = (c - 2) * 128
                                P.dma("sp", sod.K((c, half))[r0:r0 + 128, half * 512:(half + 1) * 512], t[:, :])
            with Scope(P) as S2:
                wo = self.load_wo(S2, "mlstm_w_o")
                mng = S2.sb("mng", [128, D], F32, slot="ld2")
                P.dma("sp", mng[:, :], I["mng_rep"])
                mk = S2.sb("mk", [8, 18, 128], F32, slot="ld3")
                P.dma("sp", mk[:, :, :], I["scan_mask"])
                a8 = S2.sb("a8", [128, NCH, 8], F32)
                AA = S2.sb("AA", [128, NCH, 16], F32)
                bias8 = S2.sb("bias8", [128, NCH, 8], F32)
                e8 = S2.sb("e8", [128, NCH, 8], F32)
                xA8 = S2.sb("xA8", [128, NCH, 8], F32)
                dec8 = S2.sb("dec8", [128, NCH, 8], F32)
                Atot = S2.sb("Atot", [128, 8], F32)
                Cst = S2.sb("Cst", [128, 8, 257], F32)
                Cbf = S2.sb("Cbf", [128, 8, 257], BF16)
                Cctx = S2.sb("Cctx", [128, 8, 257], F32)
                Cg = S2.sb("Cg", [128, 8, 257], F32, slot="ld0")
                At = S2.sb("At", [8, 8], F32, slot="ld1")
                Wt = S2.sb("Wt", [128, 18, 4], F32)
                abc = [S2.sb(f"abc{i}", [128, 128], F32) for i in range(2)]
                Dt = [S2.sb(f"Dt{i}", [128, 128], F32) for i in range(2)]
                Ws = [S2.sb(f"Ws{i}", [128, 128], BF16) for i in range(2)]
                ke = [S2.sb(f"ke{i}", [128, 128], BF16) for i in range(2)]
                Hs = [S2.sb(f"Hs{i}", [128, 257], F32) for i in range(2)]
                dn = [S2.sb(f"dn{i}", [128, 2], F32) for i in range(2)]
                hfw = [S2.sb(f"hfw{i}", [128, 256], F32) for i in range(2)]
                vx = [S2.sb(f"vx{i}", [128, 4, 257], BF16, slot=f"lv{i}") for i in range(3)]
                htot = S2.sb("htot", [128, D], F32)
                hf = S2.sb("hf", [128, D], F32, slot="ld4")
                so = S2.sb("so", [128, D], BF16, slot="ld5")
                yk = S2.sb("yk", [128, D], BF16)
                yTc = S2.sb("yTc", [128, 8, 128], BF16)
                junk = S2.sb("junk", [128, 256], F32)
                ssq = S2.sb("ssq", [128, 4], F32)
                csb = self.dram("ml_csb", [8 * 128, 257], F32, "dr3")
                csg = self.dram("ml_csg", [NCORES * 8 * 128, 257], F32, "cgc", unit=1)
                atb = self.dram("ml_atb", [1, 8], F32, "dr4")
                atg = self.dram("ml_atg", [NCORES, 8], F32, "cga", unit=1)
                P.act(a8[:, :, 0:4], Gt[:, :, 4:8], AF.Exp, scale=-1.0)
                P.act(a8[:, :, 4:8], Gt[:, :, 12:16], AF.Exp, scale=-1.0)
                P.ts("dve", a8[:, :, :], a8[:, :, :], 1.0, None, ALU.add)
                P.act(a8[:, :, :], a8[:, :, :], AF.Ln)
                P.ts("dve", a8[:, :, :], a8[:, :, :], -1.0, None, ALU.mult)
                pa = self.ps[1]
                for c in range(NCH):
                    P.matmul(pa.K((c, 0))[:, c * 16:c * 16 + 4], self.cmat[:, 2, :], a8[:, c, 0:4])
                    P.matmul(pa.K((c, 1))[:, c * 16 + 4:c * 16 + 8], self.cmat[:, 3, :], a8[:, c, 4:8])
                    P.matmul(pa.K((c, 2))[:, c * 16 + 8:c * 16 + 16], self.onesF[:, :], a8[:, c, :])
                P.copy("dve", View(AA, AA.h[:, :, :].rearrange("p c k -> p (c k)"), None), pa[:, 0:NCH * 16])
                P.tt("dve", bias8[:, :, 0:4], Gt[:, :, 0:4], AA[:, :, 0:4], ALU.subtract)
                P.tt("dve", bias8[:, :, 4:8], Gt[:, :, 8:12], AA[:, :, 4:8], ALU.subtract)
                P.tt("dve", e8[:, :, :], AA[:, :, 8:16], bias8[:, :, :], ALU.add)
                P.act(e8[:, :, :], e8[:, :, :], AF.Exp)
                P.act(xA8[:, :, :], AA[:, :, 0:8], AF.Exp)
                P.act(dec8[:, :, :], AA[:, :, 8:16], AF.Exp)
                P.op("dve", lambda e: e.tensor_reduce(out=Atot.h[:, :], in_=AA.h[:, 2:NCH, 8:16].rearrange("p c k -> p k c"),
                                                      axis=AX.X, op=ALU.add),
                     reads=(AA[:, :, :],), writes=(Atot[:, :],))
                P.memset("dve", Cst[:, :, :], 0.0)
                P.memset("pool", Cbf[:, :, :], 0.0)
                cnt = [0]
                vxi = [0]

                def load_vx(c):
                    t = vx[vxi[0] % 3]
                    vxi[0] += 1
                    P.dma("sp", View(t, t.h[:, :, :].rearrange("p a b -> p (a b)"), None), vd[c * 128:(c + 1) * 128, :])
                    return t

                def step(c, d, hh, vxt, outputs, inter=True):
                    col = d * 4 + hh
                    u = cnt[0] % 2
                    cnt[0] += 1
                    kch = kT[:, hh, c * 128:(c + 1) * 128]
                    if outputs:
                        qch = qT[:, hh, (c - 2) * 128:(c - 1) * 128]
                        P.ts("dve", abc[u][:, :], self.onesF[:, :], a8[:, c, col:col + 1], None, ALU.mult)
                        pA = self.ps[1]
                        P.matmul(pA.K("A")[:, 0:128], abc[u][:, :], self.cmat[:, 2 + d, :], start=True, stop=False)
                        P.matmul(pA.K("A")[:, 0:128], self.cmat[:, 0, :], self.cmat[:, 4 + d, :], start=False, stop=True)
                        P.act(Dt[u][:, :], pA.K("A")[:, 0:128], AF.Exp, bias=bias8[:, c, col:col + 1])
                        pS = self.ps[5]
                        P.matmul(pS[:, 0:128], kch, qch)
                        P.tt("dve", Ws[u][:, :], pS[:, 0:128], Dt[u][:, :], ALU.mult)
                        pH = self.ps[2]
                        P.matmul(pH[:, 0:257], Ws[u][:, :], vxt[:, hh, :])
                        P.copy("act", Hs[u][:, :], pH[:, 0:257])
                        if inter:
                            pI = self.ps[3]
                            P.matmul(pI[:, 0:257], qch, Cbf[:, col, :])
                            P.stt("dve", Hs[u][:, :], pI[:, 0:257], xA8[:, c, col:col + 1], Hs[u][:, :], ALU.mult, ALU.add)
                        P.act(dn[u][:, 0:1], Hs[u][:, 256:257], AF.Abs)
                        P.ts("dve", dn[u][:, 0:1], dn[u][:, 0:1], 1.0, None, ALU.max)
                        P.recip(dn[u][:, 1:2], dn[u][:, 0:1])
                        if d == 0:
                            P.ts("dve", hfw[u][:, :], Hs[u][:, 0:256], dn[u][:, 1:2], None, ALU.mult)
                            r0 = (c - 2) * 128
                            P.dma("sp", hfd.K((c, hh))[r0:r0 + 128, hh * 256:(hh + 1) * 256], hfw[u][:, :])
                        else:
                            P.ts("dve", htot.K(hh)[:, hh * 256:(hh + 1) * 256], Hs[u][:, 0:256], dn[u][:, 1:2], None, ALU.mult)
                    pk = self.psb2
                    P.transpose(pk.K(hh)[:, hh * 128:(hh + 1) * 128], kch, self.identB[:, :])
                    P.ts("dve", ke[u][:, :], pk.K(hh)[:, hh * 128:(hh + 1) * 128], e8[:, c, col:col + 1], None, ALU.mult)
                    pC = self.ps[0]
                    P.matmul(pC[:, 0:257], ke[u][:, :], vxt[:, hh, :])
                    P.stt("dve", Cst.K(col)[:, col, :], Cst.K(col)[:, col, :], dec8[:, c, col:col + 1], pC[:, 0:257],
                          ALU.mult, ALU.add)
                    P.copy("act", Cbf.K(col)[:, col, :], Cst.K(col)[:, col, :])

                orders = {0: list(range(NCH)), 1: [1, 0] + list(range(NCH - 1, 1, -1))}
                for i in range(NCH):
                    for d in range(2):
                        c = orders[d][i]
                        vxt = load_vx(c)
                        for hh in range(4):
                            step(c, d, hh, vxt, False)
                    if i == 1:
                        P.copy("dve", Cctx[:, :, :], Cst[:, :, :])
                        P.memset("dve", Cst[:, :, :], 0.0)
                P.dma("sp", View(csb, csb.h.rearrange("(c p) f -> p c f", p=128), None), Cst[:, :, :])
                P.dma("sp", atb[:, :], Atot[0:1, :])
                P.collective(csg[:, :], csb[:, :])
                P.collective(atg[:, :], atb[:, :])
                P.dma("sp", At[:, :], atg[:, :])
                pw = self.ps[4]
                for d in range(2):
                    for x in range(9):
                        k = d * 9 + x
                        P.matmul(pw.K(k)[:, k * 4:k * 4 + 4], mk[:, k, :], At[:, d * 4:(d + 1) * 4])
                P.act(View(Wt, Wt.h[:, :, :].rearrange("p a b -> p (a b)"), None), pw[:, 0:72], AF.Exp)
                for hh in range(4):
                    P.tt("dve", Wt[:, :, hh], Wt[:, :, hh], sm[:, self.SM_VALID:self.SM_VALID + 18], ALU.mult)
                csv = csg.h.rearrange("(r c p) f -> p r c f", c=8, p=128)
                for col in range(8):
                    d, hh = divmod(col, 4)
                    P.dma("sp", Cg[:, :, :], View(csg, csv[:, :, col, :], None))
                    P.ts("dve", Cst.K(col)[:, col, :], Cctx[:, col, :], Wt[:, d * 9 + 8, hh:hh + 1], None, ALU.mult)
                    for rp in range(NCORES):
                        P.stt("dve", Cst.K(col)[:, col, :], Cg[:, rp, :], Wt[:, d * 9 + rp, hh:hh + 1], Cst.K(col)[:, col, :],
                              ALU.mult, ALU.add)
                    P.copy("act", Cbf.K(col)[:, col, :], Cst.K(col)[:, col, :])
                for c in range(2, NCH):
                    vxt = load_vx(c)
                    for hh in range(4):
                        step(c, 0, hh, vxt, True)
                for c in range(NCH - 1, 1, -1):
                    vxt = load_vx(c)
                    for hh in range(4):
                        step(c, 1, hh, vxt, True)
                    r0 = (c - 2) * 128
                    P.dma("sp", hf[:, :], hfd[r0:r0 + 128, :])
                    P.dma("sp", so[:, :], sod[r0:r0 + 128, :])
                    P.tt("pool", htot[:, :], htot[:, :], hf[:, :], ALU.add)
                    P.memset("pool", ssq[:, :], 0.0)
                    for hh in range(4):
                        P.act(junk[:, :], htot[:, hh * 256:(hh + 1) * 256], AF.Square, accum=ssq.K(hh)[:, hh:hh + 1])
                    P.ts("dve", ssq[:, :], ssq[:, :], 1.0 / 256.0, EPS, ALU.mult, ALU.add)
                    P.act(ssq[:, :], ssq[:, :], AF.Ln)
                    P.act(ssq[:, :], ssq[:, :], AF.Exp, scale=-0.5)
                    for hh in range(4):
                        P.ts("dve", htot.K(hh)[:, hh * 256:(hh + 1) * 256], htot[:, hh * 256:(hh + 1) * 256], ssq[:, hh:hh + 1],
                             None, ALU.mult)
                    P.tt("pool", htot[:, :], htot[:, :], mng[:, :], ALU.mult)
                    P.tt("dve", yk[:, :], htot[:, :], so[:, :], ALU.mult)
                    pt = self.psb
                    for fc in range(8):
                        P.transpose(pt.K(fc)[:, fc * 128:(fc + 1) * 128], yk[:, fc * 128:(fc + 1) * 128], self.identB[:, :])
                    P.copy("act", View(yTc, yTc.h[:, :, :].rearrange("p a b -> p (a b)"), None), pt[:, :])
                    bi, off = chunk_src(c)
                    dst = self.lat[bi]
                    for oc in range(8):
                        po = self.ps[3 + oc % 2]
                        for fc in range(8):
                            P.matmul(po[:, 0:128], wo[:, fc, oc * 128:(oc + 1) * 128], yTc[:, fc, :],
                                     start=(fc == 0), stop=(fc == 7))
                        P.stt("dve", dst.K(oc)[:, oc, off:off + 128], po[:, 0:128], self.dmv(1, 2, oc, 0),
                              dst.K(oc)[:, oc, off:off + 128], ALU.mult, ALU.add)


    def finish(self):
        P = self.P
        out_tt = TT("outT", self.outT, P.slot("out"))
        dbg_tt = TT("dbgT", self.dbgT, P.slot("dbg"))
        ov = self.outT.rearrange("(kc p) t -> p kc t", p=128)
        for b in range(NB):
            P.dma("sp", View(out_tt, ov[:, :, b * BN:(b + 1) * BN], b), self.lat[b][:, :, :])
        P.dma("sp", View(dbg_tt, self.dbgT.rearrange("(kc p) t -> p kc t", p=128), None), self.cx[:, :, :])
        P.op("sp", None, reads=(View(out_tt, None, None), View(dbg_tt, None, None)))


def _consts():
    ident = np.eye(128, dtype=np.float32)
    rot = np.zeros((128, 128), np.float32)
    for d in range(128):
        a, r = divmod(d, 64)
        b, c = divmod(r, 32)
        if b == 0:
            rot[a * 64 + 32 + c, d] = -1.0
        else:
            rot[a * 64 + c, d] = 1.0
    s = np.arange(128)[:, None]
    t = np.arange(128)[None, :]
    triU = (s <= t).astype(np.float32)
    triL = (s >= t).astype(np.float32)
    negU = np.where(s <= t, 0.0, -30000.0).astype(np.float32)
    negL = np.where(s >= t, 0.0, -30000.0).astype(np.float32)
    return np.ascontiguousarray(np.stack([ident, rot, triU, triL, negU, negL], axis=1).astype(np.float32))


def _rope_tables():
    rows = SEQ // GRID_W
    row = np.repeat(np.arange(rows), GRID_W).astype(np.float32)
    col = np.tile(np.arange(GRID_W), rows).astype(np.float32)
    seg = DH // 2
    inv = (np.float32(10000.0) ** (-np.arange(seg // 2, dtype=np.float32) / np.float32(seg // 2))).astype(np.float32)
    ang_r = row[:, None] * inv
    ang_c = col[:, None] * inv
    ang = np.concatenate([ang_r, ang_r, ang_c, ang_c], axis=-1).astype(np.float32)
    return np.cos(ang).astype(np.float32), np.sin(ang).astype(np.float32)


def _fm(v):
    v = np.asarray(v, np.float32)
    lead = v.shape[:-1]
    r = v.reshape(lead + (8, 128))
    r = np.moveaxis(r, -1, 0)
    return np.ascontiguousarray(r)


_CACHE = {}


def make_in_maps(inp):
    f = lambda k: np.ascontiguousarray(np.asarray(inp[k], np.float32))
    x = f("x")[0]
    ctx = f("ctx")[0]
    cc = np.stack([f("c")[0], f("c_ctx")], axis=-1)
    mod_b = f("mod_b")
    modbT = np.ascontiguousarray(mod_b.reshape(DEPTH, 72, 128).transpose(2, 0, 1))
    ngT = _fm(f("norm_g"))
    conv_kT = _fm(f("conv_k"))
    qkgT = np.ascontiguousarray(np.stack([f("attn_q_g")[0], f("attn_k_g")[0]], axis=-1))
    bg = f("mlstm_b_gate")[0].reshape(-1)
    bgate_rep = np.ascontiguousarray(np.broadcast_to(bg[None, :], (128, 16)))
    mng_rep = np.ascontiguousarray(np.broadcast_to(f("mlstm_norm_g")[0][None, :], (128, D)))
    cos, sin = _rope_tables()
    cmat = _consts()
    shared = {"ctxT": np.ascontiguousarray(ctx.T), "modbT": modbT, "ngT": ngT, "conv_kT": conv_kT, "qkgT": qkgT,
              "bgate_rep": bgate_rep, "mng_rep": mng_rep, "cmat": cmat}
    wfull = {}
    for n, nm, r, c in Builder.WSPEC:
        wfull[n] = f(n).reshape(nm, NCORES, r // NCORES, c)
    mod_w = f("mod_w").reshape(DEPTH, NCORES, 128, 9 * D)
    maps = []
    for r in range(NCORES):
        m = dict(shared)
        sl = slice(r * TPC, (r + 1) * TPC)
        m["xT"] = np.ascontiguousarray(x[sl].T)
        m["cosT"] = np.ascontiguousarray(cos[sl].T)
        m["sinT"] = np.ascontiguousarray(sin[sl].T)
        m["cc_own"] = np.ascontiguousarray(cc[r * 128:(r + 1) * 128])
        m["mod_w"] = np.ascontiguousarray(mod_w[:, r])
        for n in wfull:
            m[n] = np.ascontiguousarray(wfull[n][:, r])
        hs = np.zeros((128, 2, 16), np.float32)
        if r > 0:
            hs[:, 0, (r - 1) * 2 + 1] = 1.0
        if r < NCORES - 1:
            hs[:, 1, (r + 1) * 2 + 0] = 1.0
        m["halo_sel"] = hs
        sm = np.zeros((8, 18, 128), np.float32)
        sv = np.zeros((128, 18), np.float32)
        for d in range(2):
            before = list(range(0, r)) if d == 0 else list(range(NCORES - 1, r, -1))
            for idx, rp in enumerate(before):
                for rpp in before[idx + 1:]:
                    sm[rpp, d * 9 + rp, :] = 1.0
                sv[:, d * 9 + rp] = 1.0
            for rpp in before:
                sm[rpp, d * 9 + 8, :] = 1.0
            sv[:, d * 9 + 8] = 1.0
        m["scan_mask"] = sm
        m["scan_valid"] = sv
        maps.append(m)
    return maps


def run(inp, stop=None):
    key = ("nc", stop)
    if key not in _CACHE:
        _CACHE[key] = Builder(stop).build()
    nc = _CACHE[key]
    maps = make_in_maps(inp)
    res = run_bass_kernel_spmd(nc, maps, core_ids=list(range(NCORES)))
    outs = [np.asarray(r["outT"]) for r in res.results]
    lat = np.concatenate([o.T for o in outs], axis=0)[None]
    dbg = [np.asarray(r["dbgT"]).T for r in res.results]
    return np.ascontiguousarray(lat.astype(np.float32)), dbg


def kernel(**inputs):
    lat, _ = run(inputs, None)
    return lat
```
